# Optimizing a Trainium2 kernel written in Bass

```python
import math
import jax, jax.numpy as jnp
from jax import lax
import numpy as np

D_MODEL = 2048
BATCH = 4
SEQ = 4096
DEPTH = 2

PLE_DIM = 256
ROPE_THETA = 10000.0
EPS = 1e-6
Q_BLOCK = 128
DIFF_QK_DIM = 64
DIFF_V_DIM = 2 * DIFF_QK_DIM
DIFF_WIDTH = D_MODEL // 4
DIFF_HEADS = DIFF_WIDTH // DIFF_V_DIM
DSA_HEAD_DIM = 128
DSA_WIDTH = D_MODEL // 4
DSA_HEADS = DSA_WIDTH // DSA_HEAD_DIM
IDX_HEADS = 8
IDX_DIM = 64
DSA_TOPK_MAX = 256
SSM_WIDTH = D_MODEL - DIFF_WIDTH - DSA_WIDTH
SSM_GROUP = 16
SSM_GROUPS = SSM_WIDTH // SSM_GROUP
SSM_STATE = 64
D_FF = 4 * D_MODEL

IN_SPLITS = (
    DIFF_HEADS * 2 * DIFF_QK_DIM,
    DIFF_HEADS * 2 * DIFF_QK_DIM,
    DIFF_WIDTH,
    DSA_HEADS * DSA_HEAD_DIM,
    DSA_HEAD_DIM,
    DSA_HEAD_DIM,
    IDX_HEADS * IDX_DIM,
    IDX_DIM,
    IDX_HEADS,
    SSM_WIDTH,
)
IN_WIDTH = sum(IN_SPLITS)
SPLIT_IDX = [int(v) for v in np.cumsum(IN_SPLITS)[:-1]]

kernel_name = "hymba_style_diff_dsa_s5_hybrid"


def rmsnorm(x, g):
    xf = x.astype(jnp.float32)
    y = xf * lax.rsqrt(jnp.mean(xf * xf, axis=-1, keepdims=True) + EPS)
    return (y * g.astype(jnp.float32)).astype(x.dtype)


def rope(x, positions):
    d = x.shape[-1]
    freqs = ROPE_THETA ** (-jnp.arange(0, d, 2, dtype=jnp.float32) / d)
    ang = positions.astype(jnp.float32)[..., None] * freqs
    ang = ang.reshape(ang.shape[:2] + (1,) * (x.ndim - 3) + (d // 2,))
    cos, sin = jnp.cos(ang).astype(x.dtype), jnp.sin(ang).astype(x.dtype)
    x1, x2 = x[..., : d // 2], x[..., d // 2:]
    return jnp.concatenate([x1 * cos - x2 * sin, x2 * cos + x1 * sin], axis=-1)


def to_blocks(a):
    b, l = a.shape[:2]
    return jnp.moveaxis(a.reshape((b, l // Q_BLOCK, Q_BLOCK) + a.shape[2:]), 1, 0)


def from_blocks(a):
    nb, b, q = a.shape[:3]
    return jnp.moveaxis(a, 0, 1).reshape((b, nb * q) + a.shape[3:])


def diff_attention(q, k, v, positions, lq1, lk1, lq2, lk2, subln_g, lambda_init):
    b, l, _ = q.shape
    q = rope(q.reshape(b, l, DIFF_HEADS, 2, DIFF_QK_DIM), positions)
    k = rope(k.reshape(b, l, DIFF_HEADS, 2, DIFF_QK_DIM), positions)
    v = v.reshape(b, l, DIFF_HEADS, DIFF_V_DIM)
    f32 = jnp.float32
    lam = (jnp.exp(jnp.sum(lq1.astype(f32) * lk1.astype(f32)))
           - jnp.exp(jnp.sum(lq2.astype(f32) * lk2.astype(f32))) + lambda_init)
    scale = DIFF_QK_DIM ** -0.5
    kpos = jnp.arange(l)

    def block(args):
        qb, t0 = args
        logits = jnp.einsum('bqhcd,bshcd->bhcqs', qb, k).astype(f32) * scale
        causal = kpos[None, :] <= (t0 + jnp.arange(Q_BLOCK))[:, None]
        pr = jax.nn.softmax(jnp.where(causal, logits, -jnp.inf), axis=-1)
        attn = pr[:, :, 0] - lam * pr[:, :, 1]
        return jnp.einsum('bhqs,bshe->bqhe', attn.astype(v.dtype), v)

    nb = l // Q_BLOCK
    out = from_blocks(lax.map(block, (to_blocks(q), jnp.arange(nb) * Q_BLOCK)))
    out = rmsnorm(out, subln_g) * (1.0 - lambda_init)
    return out.reshape(b, l, DIFF_WIDTH)


def dsa_attention(q, k, v, iq, ik, iw, positions):
    b, l, _ = q.shape
    f32 = jnp.float32
    q = rope(q.reshape(b, l, DSA_HEADS, DSA_HEAD_DIM), positions)
    k = rope(k, positions)
    iq = rope(iq.reshape(b, l, IDX_HEADS, IDX_DIM), positions)
    ik = rope(ik, positions)
    iw = iw * (IDX_HEADS ** -0.5)
    topk = min(DSA_TOPK_MAX, l // 4)
    scale = DSA_HEAD_DIM ** -0.5
    idx_scale = IDX_DIM ** -0.5
    kpos = jnp.arange(l)
    gather = jax.vmap(lambda arr, ind: arr[ind])

    def block(args):
        qb, iqb, iwb, t0 = args
        qi = t0 + jnp.arange(Q_BLOCK)
        rel = jax.nn.relu(jnp.einsum('bqhd,bsd->bqhs', iqb, ik) * idx_scale)
        score = jnp.einsum('bqhs,bqh->bqs', rel, iwb).astype(f32)
        score = jnp.where(kpos[None, None, :] <= qi[None, :, None], score, -jnp.inf)
        _, sel = lax.top_k(score, topk)
        kg = gather(k, sel)
        vg = gather(v, sel)
        logits = jnp.einsum('bqhd,bqkd->bhqk', qb, kg).astype(f32) * scale
        valid = (sel <= qi[None, :, None])[:, None]
        pr = jax.nn.softmax(jnp.where(valid, logits, -jnp.inf), axis=-1)
        return jnp.einsum('bhqk,bqkd->bqhd', pr.astype(vg.dtype), vg)

    nb = l // Q_BLOCK
    out = lax.map(block, (to_blocks(q), to_blocks(iq), to_blocks(iw), jnp.arange(nb) * Q_BLOCK))
    return from_blocks(out).reshape(b, l, DSA_WIDTH)


def s5_glu(u, lam_re, lam_im, log_step, b_re, b_im, c_re, c_im, d_skip, w_glu):
    bsz, l, _ = u.shape
    f32 = jnp.float32
    ug = u.astype(f32).reshape(bsz, l, SSM_GROUPS, SSM_GROUP)
    step = jnp.exp(log_step.astype(f32))[:, None]
    lr, li = lam_re.astype(f32), lam_im.astype(f32)
    mag = jnp.exp(lr * step)
    ab_re, ab_im = mag * jnp.cos(li * step), mag * jnp.sin(li * step)
    den = lr * lr + li * li
    nr, ni = ab_re - 1.0, ab_im
    f_re, f_im = (nr * lr + ni * li) / den, (ni * lr - nr * li) / den
    br, bi = b_re.astype(f32), b_im.astype(f32)
    bb_re = f_re[..., None] * br - f_im[..., None] * bi
    bb_im = f_re[..., None] * bi + f_im[..., None] * br
    bu_re = jnp.einsum('blgc,gpc->blgp', ug, bb_re)
    bu_im = jnp.einsum('blgc,gpc->blgp', ug, bb_im)
    a_re = jnp.broadcast_to(ab_re, (1, l, SSM_GROUPS, SSM_STATE))
    a_im = jnp.broadcast_to(ab_im, (1, l, SSM_GROUPS, SSM_STATE))

    def combine(e1, e2):
        a1r, a1i, b1r, b1i = e1
        a2r, a2i, b2r, b2i = e2
        return (a2r * a1r - a2i * a1i, a2r * a1i + a2i * a1r,
                a2r * b1r - a2i * b1i + b2r, a2r * b1i + a2i * b1r + b2i)

    _, _, xr, xi = lax.associative_scan(combine, (a_re, a_im, bu_re, bu_im), axis=1)
    y = (jnp.einsum('blgp,gcp->blgc', xr, c_re.astype(f32))
         - jnp.einsum('blgp,gcp->blgc', xi, c_im.astype(f32))
         + d_skip.astype(f32).reshape(SSM_GROUPS, SSM_GROUP) * ug)
    y = jax.nn.gelu(y.reshape(bsz, l, SSM_WIDTH))
    y = y * jax.nn.sigmoid(y @ w_glu.astype(f32))
    return y.astype(u.dtype)


def hybrid_layer(h, p_i, positions, lambda_init, norm_mix_g, w_in, w_out,
                 lq1, lk1, lq2, lk2, subln_g,
                 lam_re, lam_im, log_step, b_re, b_im, c_re, c_im, d_skip, w_glu,
                 norm_mlp_g, w_up, w_down, norm_ple_g, w_ple_gate, w_ple_proj):
    xn = rmsnorm(h, norm_mix_g)
    z = xn @ w_in
    a_q, a_k, a_v, b_q, b_k, b_v, i_q, i_k, i_w, c_u = jnp.split(z, SPLIT_IDX, axis=-1)
    y_a = diff_attention(a_q, a_k, a_v, positions, lq1, lk1, lq2, lk2, subln_g, lambda_init)
    y_b = dsa_attention(b_q, b_k, b_v, i_q, i_k, i_w, positions)
    y_c = s5_glu(c_u, lam_re, lam_im, log_step, b_re, b_im, c_re, c_im, d_skip, w_glu)
    h = h + jnp.concatenate([y_a, y_b, y_c], axis=-1) @ w_out
    hid = jnp.square(jax.nn.relu(rmsnorm(h, norm_mlp_g) @ w_up))
    h = h + hid @ w_down
    gate = jax.nn.sigmoid(rmsnorm(h, norm_ple_g) @ w_ple_gate)
    return h + gate * (p_i @ w_ple_proj)


def setup_inputs(seed: int = 0) -> dict:
    key = jax.random.key(seed)
    ks = iter(jax.random.split(key, 40))

    def nrm(shape, scale):
        return jax.random.normal(next(ks), shape, jnp.float32) * scale

    def gain(shape):
        return 1.0 + nrm(shape, 0.02)

    G, P, C = SSM_GROUPS, SSM_STATE, SSM_GROUP
    lam_im = (jnp.pi * jnp.arange(P, dtype=jnp.float32))[None, None, :] + nrm((DEPTH, G, P), 0.01)
    return {
        "x": nrm((BATCH, SEQ, D_MODEL), 1.0),
        "p": nrm((DEPTH, BATCH, SEQ, PLE_DIM), 1.0),
        "positions": jnp.broadcast_to(jnp.arange(SEQ, dtype=jnp.int32), (BATCH, SEQ)),
        "norm_mix_g": gain((DEPTH, D_MODEL)),
        "w_in": nrm((DEPTH, D_MODEL, IN_WIDTH), D_MODEL ** -0.5),
        "w_out": nrm((DEPTH, D_MODEL, D_MODEL), D_MODEL ** -0.5),
        "diff_lq1": nrm((DEPTH, DIFF_QK_DIM), 0.1),
        "diff_lk1": nrm((DEPTH, DIFF_QK_DIM), 0.1),
        "diff_lq2": nrm((DEPTH, DIFF_QK_DIM), 0.1),
        "diff_lk2": nrm((DEPTH, DIFF_QK_DIM), 0.1),
        "diff_subln_g": gain((DEPTH, DIFF_V_DIM)),
        "ssm_lambda_re": -0.5 + nrm((DEPTH, G, P), 0.01),
        "ssm_lambda_im": lam_im,
        "ssm_log_step": jax.random.uniform(next(ks), (DEPTH, G), jnp.float32,
                                           math.log(1e-3), math.log(1e-1)),
        "ssm_B_re": nrm((DEPTH, G, P, C), (2.0 * C) ** -0.5),
        "ssm_B_im": nrm((DEPTH, G, P, C), (2.0 * C) ** -0.5),
        "ssm_C_re": nrm((DEPTH, G, C, P), (2.0 * P) ** -0.5),
        "ssm_C_im": nrm((DEPTH, G, C, P), (2.0 * P) ** -0.5),
        "ssm_D": nrm((DEPTH, SSM_WIDTH), 1.0),
        "ssm_w_glu": nrm((DEPTH, SSM_WIDTH, SSM_WIDTH), SSM_WIDTH ** -0.5),
        "norm_mlp_g": gain((DEPTH, D_MODEL)),
        "w_up": nrm((DEPTH, D_MODEL, D_FF), D_MODEL ** -0.5),
        "w_down": nrm((DEPTH, D_FF, D_MODEL), D_FF ** -0.5),
        "norm_ple_g": gain((DEPTH, D_MODEL)),
        "w_ple_gate": nrm((DEPTH, D_MODEL, D_MODEL), D_MODEL ** -0.5),
        "w_ple_proj": nrm((DEPTH, PLE_DIM, D_MODEL), PLE_DIM ** -0.5),
        "final_g": gain((D_MODEL,)),
    }


def reference(x, p, positions, norm_mix_g, w_in, w_out, diff_lq1, diff_lk1, diff_lq2, diff_lk2,
              diff_subln_g, ssm_lambda_re, ssm_lambda_im, ssm_log_step, ssm_B_re, ssm_B_im,
              ssm_C_re, ssm_C_im, ssm_D, ssm_w_glu, norm_mlp_g, w_up, w_down, norm_ple_g,
              w_ple_gate, w_ple_proj, final_g):
    h = x
    for i in range(DEPTH):
        lambda_init = 0.8 - 0.6 * math.exp(-0.3 * i)
        h = hybrid_layer(h, p[i], positions, lambda_init, norm_mix_g[i], w_in[i], w_out[i],
                         diff_lq1[i], diff_lk1[i], diff_lq2[i], diff_lk2[i], diff_subln_g[i],
                         ssm_lambda_re[i], ssm_lambda_im[i], ssm_log_step[i], ssm_B_re[i],
                         ssm_B_im[i], ssm_C_re[i], ssm_C_im[i], ssm_D[i], ssm_w_glu[i],
                         norm_mlp_g[i], w_up[i], w_down[i], norm_ple_g[i], w_ple_gate[i],
                         w_ple_proj[i])
    return rmsnorm(h, final_g)
```

```python
import math
import numpy as np
from contextlib import ExitStack
import concourse.bass as bass
import concourse.mybir as mybir
from concourse.bass_utils import run_bass_kernel_spmd

F32 = mybir.dt.float32
BF16 = mybir.dt.bfloat16
I32 = mybir.dt.int32
AF = mybir.ActivationFunctionType
ALU = mybir.AluOpType
AX = mybir.AxisListType

D = 2048
T = 4096
TT = 512
NT = T // TT
TL = 2048
NTL = TL // TT
G = ((0, 3, 4, 7), (1, 2, 5, 6))
RG = (0, 1, 1, 0, 0, 1, 1, 0)
LG = (0, 0, 1, 1, 2, 2, 3, 3)
ZR = 3264
RCH = 512
VCH = 1024
NZC = (ZR + RCH - 1) // RCH
ZGR = 2 * RCH * NZC


def zg_pieces(q, r0, n):
    out = []
    off = 0
    while n > 0:
        k = r0 // RCH
        m = min(n, (k + 1) * RCH - r0)
        rows_k = min(RCH, ZR - k * RCH)
        out.append((k * 2 * RCH + q * rows_k + (r0 - k * RCH), m, off))
        r0 += m
        off += m
        n -= m
    return out


def vg_row(q, r0):
    k = r0 // VCH
    return k * 2 * VCH + q * VCH + (r0 - k * VCH)
PAIRS = [[0, 1], [2, 3], [4, 5], [6, 7]]
KC = D // 128
INW = 3912
DFF = 8192
EPS = 1e-6
AQ, AK, AV, BQ, BK, BV, IQ, IK, IW, CU = 0, 512, 1024, 1536, 2048, 2176, 2304, 2816, 2880, 2888
fAQ, fAK, fBQ, fBK, fIQ, fIK, fCU = 0, 512, 1024, 1536, 1664, 2176, 2240
vAV, vBV, vIW = 3264, 3776, 3904
NVEC = 16 * 4 + 8 + 1
TWO_PI_S = 6.28318
NEG = -3.0e38


class Tk:
    __slots__ = ("w", "r")

    def __init__(self):
        self.w = None
        self.r = []


def toks(n):
    return [Tk() for _ in range(n)]


class Op:
    __slots__ = ("eng", "fn", "dma", "deps", "signal", "sem", "val", "slotwait", "pos", "cc")

    def __init__(self, eng, fn, dma):
        self.pos = 0
        self.cc = False
        self.eng = eng
        self.fn = fn
        self.dma = dma
        self.deps = []
        self.signal = False
        self.sem = None
        self.val = 0
        self.slotwait = None


class Sched:
    COMPUTE = ("pe", "act", "dve", "pool")
    QUEUES = ("pe", "act", "dve", "pool", "sp")
    NSLOT = 8

    def __init__(self, nc):
        self.nc = nc
        self.q = {e: [] for e in self.QUEUES}
        self.final = []
        self.bar = []
        self.bar_epoch = 0
        self.seen = {e: 0 for e in self.QUEUES}
        self.npos = 0
        self.bar_pos = 0

    def barrier(self):
        lasts = []
        for e in self.QUEUES:
            ops = self.q[e]
            comp = [o for o in ops[-64:] if not o.dma]
            if comp:
                lasts.append(comp[-1])
            dm = [o for o in ops if o.dma and not o.cc][-self.NSLOT:]
            lasts.extend(dm)
            lasts.extend(o for o in ops if o.cc and o.pos > self.bar_pos)
        self.bar = lasts
        self.bar_pos = self.npos
        self.bar_epoch += 1

    def op(self, eng, fn, r=(), w=(), dma=False):
        o = Op(eng, fn, dma)
        deps = {}
        for t in r:
            if t.w is not None:
                deps[id(t.w)] = t.w
        for t in w:
            if t.w is not None:
                deps[id(t.w)] = t.w
            for x in t.r:
                deps[id(x)] = x
        if self.seen[eng] < self.bar_epoch:
            self.seen[eng] = self.bar_epoch
            for x in self.bar:
                deps[id(x)] = x
        self.npos += 1
        o.pos = self.npos
        latest = {}
        for d in deps.values():
            if d is o:
                continue
            if (not d.dma) and (not dma) and d.eng == "pe" and eng == "pe":
                continue
            if d.dma:
                o.deps.append(d)
            else:
                if d.eng not in latest or latest[d.eng].pos < d.pos:
                    latest[d.eng] = d
        for d in latest.values():
            o.deps.append(d)
            d.signal = True
        for t in r:
            t.r.append(o)
        for t in w:
            t.w = o
            t.r = []
        self.q[eng].append(o)
        return o

    def emit(self, stack):
        nc = self.nc
        EPOCH = 30000
        nep = {e: sum(1 for o in self.q[e] if (not o.dma) and o.signal) // EPOCH + 1 for e in self.COMPUTE}
        esem = {e: [stack.enter_context(nc.semaphore("s_%s%d" % (e, i))) for i in range(nep[e])] for e in self.COMPUTE}
        dsem = {e: [stack.enter_context(nc.semaphore("d_%s%d" % (e, i))) for i in range(self.NSLOT)]
                for e in self.QUEUES}
        for e in self.QUEUES:
            cnt = 0
            nd = 0
            for o in self.q[e]:
                if o.cc:
                    o.sem = stack.enter_context(nc.semaphore("cc_%d" % o.pos))
                    o.val = 1
                    o.slotwait = (o.sem, 0)
                elif o.dma:
                    s = nd % self.NSLOT
                    o.sem = dsem[e][s]
                    o.val = 16 * (nd // self.NSLOT + 1)
                    o.slotwait = (dsem[e][s], 16 * (nd // self.NSLOT))
                    nd += 1
                elif o.signal:
                    o.sem = esem[e][cnt // EPOCH]
                    o.val = cnt % EPOCH + 1
                    cnt += 1
        block = stack.enter_context(nc.Block())
        handles = {"pe": block.tensor, "act": block.scalar, "dve": block.vector,
                   "pool": block.gpsimd, "sp": block.sync}
        final = self.final

        def make(e):
            ops = self.q[e]

            def body(eng):
                waited = {}

                def wait(sem, val):
                    if val <= 0:
                        return
                    k = id(sem)
                    if waited.get(k, 0) < val:
                        eng.wait_ge(sem, val)
                        waited[k] = val

                for o in ops:
                    for d in o.deps:
                        wait(d.sem, d.val)
                    if o.dma:
                        wait(*o.slotwait)
                    ins = o.fn(eng)
                    if o.cc:
                        ins.then_inc(o.sem)
                    elif o.dma:
                        ins.then_inc(o.sem, 16)
                    elif o.signal:
                        ins.then_inc(o.sem, 1)
                if e == "sp":
                    for o in final:
                        wait(o.sem, o.val)
            return body

        for e in self.QUEUES:
            if self.q[e] or e == "sp":
                handles[e](make(e))


class B:
    def __init__(self, nc, S):
        self.nc = nc
        self.S = S
        self.rr = 0
        self.fillregs = {}

    def mm(self, out, lhsT, rhs, start, stop, r, w):
        return self.S.op("pe", lambda e: e.matmul(out, lhsT=lhsT, rhs=rhs, start=start, stop=stop), r=r, w=w)

    def tr(self, out, in_, ident, r, w):
        return self.S.op("pe", lambda e: e.transpose(out, in_, ident), r=r, w=w)

    def act(self, out, in_, func, r, w, scale=None, bias=None, accum=None):
        kw = {}
        if scale is not None:
            kw["scale"] = scale
        if bias is not None:
            kw["bias"] = bias
        if accum is not None:
            kw["accum_out"] = accum
        return self.S.op("act", lambda e: e.activation(out=out, in_=in_, func=func, **kw), r=r, w=w)

    def tt(self, eng, out, a, b, op, r, w):
        return self.S.op(eng, lambda e: e.tensor_tensor(out=out, in0=a, in1=b, op=op), r=r, w=w)

    def ts(self, eng, out, a, s1, op0, r, w, s2=None, op1=None, accum=None):
        kw = {}
        if op1 is not None:
            kw["op1"] = op1
        if accum is not None:
            kw["accum_out"] = accum
        return self.S.op(eng, lambda e: e.tensor_scalar(out=out, in0=a, scalar1=s1, scalar2=s2, op0=op0, **kw), r=r, w=w)

    def stt(self, out, a, s, b, op0, op1, r, w):
        return self.S.op("dve", lambda e: e.scalar_tensor_tensor(out=out, in0=a, scalar=s, in1=b, op0=op0, op1=op1), r=r, w=w)

    def cp(self, eng, out, in_, r, w):
        if eng == "act":
            return self.S.op("act", lambda e: e.copy(out=out, in_=in_), r=r, w=w)
        return self.S.op(eng, lambda e: e.tensor_copy(out=out, in_=in_), r=r, w=w)

    def memset(self, eng, out, val, w):
        return self.S.op(eng, lambda e: e.memset(out, val), w=w)

    def recip(self, out, in_, r, w):
        return self.S.op("dve", lambda e: e.reciprocal(out=out, in_=in_), r=r, w=w)

    def dma(self, out, in_, r=(), w=(), q="sp", final=False, slow=False):
        kw = {"allow_slow_non_contiguous": True} if slow else {}
        o = self.S.op(q, lambda e: e.dma_start(out=out, in_=in_, **kw), r=r, w=w, dma=True)
        if final:
            self.S.final.append(o)
        return o

    def asel(self, out, in_, pattern, cmp, fill, base, cm, r, w):
        regs = self.fillregs

        def fn(e):
            if fill not in regs:
                regs[fill] = e.to_reg(fill)
            return e.affine_select(out=out, in_=in_, pattern=pattern, compare_op=cmp, fill=regs[fill],
                                   base=base, channel_multiplier=cm)
        return self.S.op("pool", fn, r=r, w=w)

    def red(self, out, in_, op, r, w):
        return self.S.op("dve", lambda e: e.tensor_reduce(out=out, in_=in_, axis=AX.X, op=op), r=r, w=w)

    def iota(self, out, pattern, r, w, cm=0):
        return self.S.op("pool", lambda e: e.iota(out, pattern=pattern, base=0, channel_multiplier=cm,
                                                  allow_small_or_imprecise_dtypes=True), r=r, w=w)

    def scan(self, out, d0, d1, init, r, w):
        return self.S.op("dve", lambda e: e.tensor_tensor_scan(out=out, data0=d0, data1=d1, initial=init,
                                                               op0=ALU.mult, op1=ALU.add), r=r, w=w)

    def allgather(self, out, in_, w=()):
        o = self.S.op("pool", lambda e: e.collective_compute("AllGather", ALU.bypass, replica_groups=PAIRS,
                                                             ins=[in_.opt()], outs=[out.opt()]), r=[], w=list(w), dma=True)
        o.cc = True
        return o

    def cast_eng(self):
        self.rr += 1
        return ("dve", "act", "dve", "act", "dve", "act", "pool")[self.rr % 7]


def build(n_layers=2, debug=None, stages=None, inject=False, dbg3=9, nqt=NT):
    debug = debug or set()
    nc = bass.Bass("TRN2", target_bir_lowering=False)

    def din(name, shape, dt=F32):
        return nc.dram_tensor(name, list(shape), dt, kind="ExternalInput").ap()

    def dscr(name, shape, dt):
        kind = "ExternalOutput" if name in debug else "Internal"
        return nc.dram_tensor(name, list(shape), dt, kind=kind).ap()

    xT = din("xT", [D, TL])
    pT = din("pT", [2, 256, TL])
    posf = din("posf", [64, TL], I32)
    qidxB_d = din("qidxB", [128, TL])
    qcol_d = din("qcol", [128, 16])
    selv_d = din("selv", [128, 2])
    freqs = din("freqs", [64, 2])
    w_in = din("w_in", [2, D, INW])
    w_out = din("w_out", [2, D, D])
    w_up = din("w_up", [2, D, DFF])
    w_down = din("w_down", [2, DFF, D])
    w_gate = din("w_gate", [2, D, D])
    w_ple = din("w_ple", [2, 256, D])
    w_glu = din("w_glu", [2, 1024, 1024])
    vecs = din("vecs", [2, 128, NVEC])
    lamv = din("lamv", [2, 128, 4, 64])
    ssmL1 = din("ssmL1", [2, 3, 128, 4, 64])
    ssmL2 = din("ssmL2", [2, 3, 128, 16])
    ssmB = din("ssmB", [2, 2, 128, 4, 64])
    ssmC = din("ssmC", [2, 2, 128, 16, 16])
    outT = nc.dram_tensor("outT", [D, TL], F32, kind="ExternalOutput").ap()

    hT = dscr("hT", [D, TL], F32)
    zF = dscr("zFl", [ZR, TL], BF16)
    zFg = dscr("zFg", [ZGR, TL], BF16)
    vtok = dscr("vtokl", [TL, 640], BF16)
    vtokg = dscr("vtokg", [2 * TL, 640], BF16)
    iwt = dscr("iwt", [TL, 8], F32)
    yprel = dscr("yprel", [512, T], F32)
    if inject:
        ycat = din("ycat", [2048, TL], BF16)
    else:
        ycat = dscr("ycat", [2048, TL], BF16)
    ypre = dscr("ypreg", [1024, T], F32)
    rope = dscr("rope", [2, 2, 64, TL], F32)
    Wi = dscr("Wi", [D, INW], BF16)
    Wo = dscr("Wo", [D, D], BF16)
    Wu = dscr("Wu", [D, DFF], BF16)
    Wd = dscr("Wd", [DFF, D], BF16)
    Wg = dscr("Wg", [D, D], BF16)
    Wp = dscr("Wp", [256, D], BF16)
    Wl = dscr("Wl", [1024, 1024], BF16)

    with ExitStack() as st:
        S = Sched(nc)
        b = B(nc, S)

        _nm = [0]

        def sb(stack, name, shape, dt):
            _nm[0] += 1
            return stack.enter_context(nc.sbuf_tensor("%s_%d" % (name, _nm[0]), list(shape), dt))

        ps = [st.enter_context(nc.psum_tensor("ps%d" % i, [128, 512], F32)) for i in range(7)]
        psx = st.enter_context(nc.psum_tensor("psx", [128, 512], F32))
        tpsx = Tk()
        tps = toks(7)
        ones = sb(st, "ones", [128, 128], BF16)
        ident = sb(st, "ident", [128, 128], BF16)
        identf = sb(st, "identf", [128, 128], F32)
        epsc = sb(st, "epsc", [128, 1], F32)
        vec = sb(st, "vec", [128, 2, NVEC], F32)
        tconst = Tk()
        b.memset("dve", ones[:], 1.0, [tconst])
        b.memset("dve", epsc[:], EPS, [tconst])
        b.memset("dve", identf[:], 1.0, [tconst])
        b.asel(identf[:], identf[:], [[-1, 128]], ALU.is_equal, 0.0, 0, 1, [], [tconst])
        b.cp("dve", ident[:], identf[:], [tconst], [tconst])
        b.dma(vec[:], vecs.rearrange("l p n -> p l n"), w=[tconst])
        qcol = sb(st, "qcol", [128, 16], F32)
        krow = sb(st, "krow", [128, 16], F32)
        selv = sb(st, "selv", [128, 2], F32)
        pio = sb(st, "pio", [128, 1], F32)
        b.dma(qcol[:], qcol_d[:, :], w=[tconst])
        b.dma(selv[:], selv_d[:, :], w=[tconst])
        b.ts("dve", krow[:], qcol[:], 1.0, ALU.add, [tconst], [tconst], s2=256.0, op1=ALU.min)
        b.iota(pio[:], [[0, 1]], [], [tconst], cm=1)

        def vcol(l, c0, n=1):
            return vec[:, l, c0:c0 + n]

        _sc_cache = {}

        def sincos(stack_, u, P, N, out_sin, out_cos, r, w, name, eng="dve"):
            if not hasattr(stack_, "sc_cache"):
                stack_.sc_cache = {}
            if N not in stack_.sc_cache:
                stack_.sc_cache[N] = (sb(stack_, name + "_ki", [128, N], I32), sb(stack_, name + "_kf", [128, N], F32),
                                      sb(stack_, name + "_rr", [128, N], F32), sb(stack_, name + "_mm", [128, N], F32), Tk())
            ki, kf, rr, mm_, tk = stack_.sc_cache[N]
            for shift, out in ((0.0, out_sin), (0.25, out_cos)):
                if shift == 0.0:
                    src = u
                    b.cp("act", ki[:P, :], u, r + [tk], [tk])
                else:
                    b.ts(eng, mm_[:P, :], u, shift, ALU.add, r + [tk], [tk])
                    src = mm_[:P, :]
                    b.cp("act", ki[:P, :], src, [tk], [tk])
                b.cp("act", kf[:P, :], ki[:P, :], [tk], [tk])
                b.tt(eng, rr[:P, :], src, kf[:P, :], ALU.subtract, r + [tk], [tk])
                b.act(out, rr[:P, :], AF.Sin, [tk], w + [tk], scale=TWO_PI_S)

        def rstd_from(stack_, name):
            sq = sb(stack_, name + "_sq", [128, 2, 4, TT], BF16)
            tsq = toks(2)
            sd = sb(stack_, name + "_sd", [128, TT], F32)
            rB = sb(stack_, name + "_rB", [128, TT], F32)
            tsd, trB = Tk(), Tk()

            def run(src, tsrc, hb, thb, psi):
                for g in range(KC // 4):
                    bi = g % 2
                    b.act(sq[:, bi, :, :], src[:, 4 * g:4 * g + 4, :], AF.Square, tsrc[4 * g:4 * g + 4], [tsq[bi]])
                    for k in range(4):
                        kc = 4 * g + k
                        b.mm(ps[psi][:], ones[:], sq[:, bi, k, :], kc == 0, kc == KC - 1, [tsq[bi], tconst], [tps[psi]])
                for kc in range(KC):
                    b.cp(("dve", "pool")[kc % 2], hb[:, kc, :], src[:, kc, :], [tsrc[kc]], [thb[kc]])
                b.act(sd[:], ps[psi][:], AF.Sqrt, [tps[psi], tconst], [tsd], scale=1.0 / D, bias=epsc[:, 0:1])
                b.recip(rB[:], sd[:], [tsd], [trB])
                return rB, trB
            return run

        with ExitStack() as s0:
            pi_ = sb(s0, "r_pi", [64, TL], I32)
            pf = sb(s0, "r_pf", [64, TL], F32)
            fr = sb(s0, "r_fr", [64, 2], F32)
            u = sb(s0, "r_u", [64, TL], F32)
            sn = sb(s0, "r_sn", [64, TL], F32)
            cs = sb(s0, "r_cs", [64, TL], F32)
            t1, t2 = Tk(), Tk()
            b.dma(pi_[:], posf[:, :], w=[t1])
            b.dma(fr[:], freqs[:, :], w=[t1])
            b.cp("dve", pf[:], pi_[:], [t1], [t1])
            for tb in range(2):
                P = 64 if tb == 0 else 32
                b.ts("dve", u[:P, :], pf[:P, :], fr[:P, tb:tb + 1], ALU.mult, [t1, t2], [t2],
                     s2=1.0 / (2 * math.pi), op1=ALU.mult)
                sincos(s0, u[:P, :], P, TL, sn[:P, :], cs[:P, :], [t2], [t2], "rsc")
                b.dma(rope[tb, 0, 0:P, :], cs[:P, :], r=[t2])
                b.dma(rope[tb, 1, 0:P, :], sn[:P, :], r=[t2])
        S.barrier()

        for L in range(n_layers):
            lam_init = 0.8 - 0.6 * math.exp(-0.3 * L)
            WIN_SEGS = [(AQ, 512, fAQ, 8), (AK, 512, fAK, 8), (BQ, 512, fBQ, 4), (BK, 128, fBK, 1),
                        (IQ, 512, fIQ, 8), (IK, 64, fIK, 1), (CU, 512, fCU, 0), (CU + 512, 512, fCU + 512, 0),
                        (AV, 512, vAV, 0), (BV, 128, vBV, 0), (IW, 8, vIW, 0)]

            def prep_list(plist, src, dst, K, segs, gcol, Lg, KG):
                nk = K // 128
                for (c0, n, d0, H) in segs:
                    for kg in range(0, nk, KG):
                        plist.append((src, dst, c0, n, d0, H, kg, min(KG, nk - kg), gcol, Lg))

            def p_load(piece, stg_i, tstg_i, q, qs=None):
                src, dst, c0, n, d0, H, kg, ng, gcol, Lg = piece
                b.dma(stg_i[:, 0:ng, 0:n], src[kg * 128:(kg + ng) * 128, c0:c0 + n].rearrange("(k p) n -> p k n", p=128),
                      w=[tstg_i], q=q)

            def p_run(piece, stg_i, tstg_i, wbf_i, twbf_i, q, engs, qs=None):
                src, dst, c0, n, d0, H, kg, ng, gcol, Lg = piece
                for k in range(ng):
                    eng = engs() if callable(engs) else engs
                    if H:
                        o_ = wbf_i[:, k, 0:n].rearrange("p (two h j) -> p two h j", two=2, h=H)
                        i_ = stg_i[:, k, 0:n].rearrange("p (h two j) -> p two h j", two=2, h=H)
                    else:
                        o_ = wbf_i[:, k, 0:n]
                        i_ = stg_i[:, k, 0:n]
                    if gcol is None:
                        if eng == "act" and H:
                            eng = "dve"
                        b.cp(eng, o_, i_, [tstg_i], [twbf_i])
                    else:
                        g = vcol(Lg, gcol + kg + k)
                        if eng == "act":
                            if H:
                                b.ts("dve", o_, i_, g, ALU.mult, [tstg_i, tconst], [twbf_i])
                            else:
                                b.act(o_, i_, AF.Copy, [tstg_i, tconst], [twbf_i], scale=g)
                        else:
                            b.ts(eng, o_, i_, g, ALU.mult, [tstg_i, tconst], [twbf_i])
                b.dma(dst[kg * 128:(kg + ng) * 128, d0:d0 + n].rearrange("(k p) n -> p k n", p=128),
                      wbf_i[:, 0:ng, 0:n], r=[twbf_i], q=(qs or q))

            if L == 0 and (stages is None or "P" in stages):
                with ExitStack() as s0:
                    NPB = 4
                    stg = [sb(s0, "p_stg%d" % i, [128, 8, 512], F32) for i in range(NPB)]
                    wbf = [sb(s0, "p_wbf%d" % i, [128, 8, 512], BF16) for i in range(NPB)]
                    tstg, twbf = toks(NPB), toks(NPB)
                    pl0 = []
                    prep_list(pl0, w_in[0], Wi, D, WIN_SEGS, 0, 0, 8)
                    p_load(pl0[0], stg[0], tstg[0], "sp")
                    p_load(pl0[1], stg[1], tstg[1], "sp")
                    for idx in range(len(pl0)):
                        if idx + 2 < len(pl0):
                            p_load(pl0[idx + 2], stg[(idx + 2) % NPB], tstg[(idx + 2) % NPB], "sp")
                        p_run(pl0[idx], stg[idx % NPB], tstg[idx % NPB], wbf[idx % NPB], twbf[idx % NPB], "sp", b.cast_eng)
                S.barrier()

            bgl = []
            if stages is None or "P" in stages:
                prep_list(bgl, w_glu[L], Wl, 1024, [(c, 512, c, 0) for c in range(0, 1024, 512)], None, L, 4)
                prep_list(bgl, w_out[L], Wo, D, [(c, 512, c, 0) for c in range(0, D, 512)], None, L, 4)
                prep_list(bgl, w_up[L], Wu, D, [(c, 512, c, 0) for c in range(0, DFF, 512)], 16, L, 4)
                prep_list(bgl, w_down[L], Wd, DFF, [(c, 512, c, 0) for c in range(0, D, 512)], None, L, 4)
                prep_list(bgl, w_gate[L], Wg, D, [(c, 512, c, 0) for c in range(0, D, 512)], 32, L, 4)
                prep_list(bgl, w_ple[L], Wp, 256, [(c, 512, c, 0) for c in range(0, D, 512)], None, L, 4)
                if L + 1 < n_layers:
                    prep_list(bgl, w_in[L + 1], Wi, D, WIN_SEGS, 0, L + 1, 4)
            bgs = {"loaded": 0, "run": 0, "bufs": None, "eng": "act", "qs": "act"}

            def bg_attach(stack_):
                nb = 3
                bgs["bufs"] = ([sb(stack_, "bg_stg%d" % i, [128, 4, 512], F32) for i in range(nb)], toks(nb),
                               [sb(stack_, "bg_wbf%d" % i, [128, 4, 512], BF16) for i in range(nb)], toks(nb))

            def bg_pump(k=1):
                stg_, tstg_, wbf_, twbf_ = bgs["bufs"]
                nb = len(stg_)
                for _ in range(k):
                    if bgs["run"] >= len(bgl):
                        return
                    while bgs["loaded"] < min(len(bgl), bgs["run"] + 2):
                        i = bgs["loaded"] % nb
                        p_load(bgl[bgs["loaded"]], stg_[i], tstg_[i], "sp")
                        bgs["loaded"] += 1
                    i = bgs["run"] % nb
                    p_run(bgl[bgs["run"]], stg_[i], tstg_[i], wbf_[i], twbf_[i], "sp", bgs["eng"], qs=bgs["qs"])
                    bgs["run"] += 1

            def bg_detach(all_=False):
                stg_, tstg_, wbf_, twbf_ = bgs["bufs"]
                nb = len(stg_)
                while bgs["run"] < (len(bgl) if all_ else bgs["loaded"]):
                    if all_:
                        bg_pump(1)
                    else:
                        i = bgs["run"] % nb
                        p_run(bgl[bgs["run"]], stg_[i], tstg_[i], wbf_[i], twbf_[i], "sp", bgs["eng"], qs=bgs["qs"])
                        bgs["run"] += 1
                bgs["bufs"] = None

            if stages is None or "1" in stages:
                src_h = xT if L == 0 else hT
                with ExitStack() as s0:
                    ht = sb(s0, "a_ht", [128, KC, TT], F32)
                    hb = sb(s0, "a_hb", [128, KC, TT], BF16)
                    tht, thb = toks(KC), toks(KC)
                    wt = [sb(s0, "a_wt%d" % i, [128, KC, 512], BF16) for i in range(2)]
                    twt = toks(2)
                    rt = sb(s0, "a_rt", [128, 2, 2, TT], F32)
                    rtR = sb(s0, "a_rtR", [128, 2, 2, TT], F32)
                    trt, trtR = Tk(), Tk()
                    tmp = [sb(s0, "a_tmp%d" % i, [128, TT], F32) for i in range(4)]
                    ttmp = toks(4)
                    ob = [sb(s0, "a_ob%d" % i, [128, TT], BF16) for i in range(4)]
                    tob = toks(4)
                    rcol = sb(s0, "a_rcol", [128, 4], F32)
                    trcol = Tk()
                    vo = [sb(s0, "a_vo%d" % i, [128, 640], BF16) for i in range(2)]
                    tvo = toks(2)
                    iwo = [sb(s0, "a_iwo%d" % i, [128, 8], F32) for i in range(2)]
                    tiwo = toks(2)
                    rs = rstd_from(s0, "a_rs")
                    wcnt = [0]
                    obc = [0]

                    FM_LOADS = [(0, 512), (512, 512), (1024, 512), (1536, 128), (1664, 512), (2176, 64), (2240, 512), (2752, 512)]
                    jobs = []

                    def rope_jobs(fbase, zbase, H, dh, table):
                        hs = H * dh // 2
                        M = min(128, hs)
                        for p_ in range(hs // M):
                            nh = M // (dh // 2)
                            jobs.append(("rope", fbase + p_ * M, fbase + hs + p_ * M, M, table, zbase, dh, nh, p_ * nh))
                    rope_jobs(fAQ, 0, 8, 64, 1)
                    rope_jobs(fAK, 512, 8, 64, 1)
                    rope_jobs(fBQ, 1024, 4, 128, 0)
                    rope_jobs(fBK, 1536, 1, 128, 0)
                    rope_jobs(fIQ, 1664, 8, 64, 1)
                    rope_jobs(fIK, 2176, 1, 64, 1)
                    for c in range(8):
                        jobs.append(("plain", fCU + c * 128, 128, 2240 + c * 128))

                    for tt_ in range(NTL):
                        t0 = tt_ * TT
                        for kc in range(KC):
                            b.dma(ht[:, kc, :], src_h[kc * 128:(kc + 1) * 128, t0:t0 + TT], w=[tht[kc]])
                        for tb in range(2):
                            nj = 64 if tb == 0 else 32
                            for rep in range(128 // nj):
                                for c_ in range(2):
                                    b.dma(rt[rep * nj:(rep + 1) * nj, tb, c_, :], rope[tb, c_, 0:nj, t0:t0 + TT], w=[trt])
                        rB, trB = rs(ht, tht, hb, thb, 6)
                        for tb in range(2):
                            for c_ in range(2):
                                b.tt("pool", rtR[:, tb, c_, :], rt[:, tb, c_, :], rB[:], ALU.mult, [trt, trB], [trtR])
                        for sbk in range(4):
                            b.tr(ps[5][:, sbk * 128:(sbk + 1) * 128], rB[:, sbk * 128:(sbk + 1) * 128], identf[:],
                                 [trB, tconst], [tps[5]])
                        b.cp("dve", rcol[:, :], ps[5][:, 0:512:128], [tps[5]], [trcol])

                        loaded = {}

                        def load_w(li):
                            c0, n = FM_LOADS[li]
                            i = wcnt[0] % 2
                            wcnt[0] += 1
                            b.dma(wt[i][:, :, 0:n], Wi[:, c0:c0 + n].rearrange("(k p) n -> p k n", p=128), w=[twt[i]])
                            loaded[li] = i

                        def lidx(col):
                            for li_, (c0_, n_) in enumerate(FM_LOADS):
                                if c0_ <= col < c0_ + n_:
                                    return li_
                            raise ValueError(col)

                        def wcols(col, M):
                            li = lidx(col)
                            return wt[loaded[li]], col - FM_LOADS[li][0]

                        def job_loads(j):
                            if j[0] == "rope":
                                return {lidx(j[1]), lidx(j[2])}
                            return {lidx(j[1])}
                        load_w(0)
                        next_load = 1
                        psi = 0
                        for j in jobs:
                            need = max(job_loads(j))
                            while next_load <= need:
                                load_w(next_load)
                                next_load += 1
                            if j[0] == "rope":
                                _, c1, c2, M, tb, zb, dh, nh, h0 = j
                                pA, pB = psi % 4, (psi + 1) % 4
                                psi += 2
                                for (col, pi2) in ((c1, pA), (c2, pB)):
                                    wtile, off = wcols(col, M)
                                    wi_ = loaded[lidx(col)]
                                    for kc in range(KC):
                                        b.mm(ps[pi2][:M, :], wtile[:, kc, off:off + M], hb[:, kc, :], kc == 0, kc == KC - 1,
                                             [twt[wi_], thb[kc]], [tps[pi2]])
                                cosR, sinR = rtR[:M, tb, 0, :], rtR[:M, tb, 1, :]
                                b.tt("dve", tmp[0][:M, :], ps[pA][:M, :], cosR, ALU.mult, [tps[pA], trtR], [ttmp[0]])
                                b.tt("dve", tmp[1][:M, :], ps[pB][:M, :], sinR, ALU.mult, [tps[pB], trtR], [ttmp[1]])
                                b.tt("dve", tmp[2][:M, :], ps[pB][:M, :], cosR, ALU.mult, [tps[pB], trtR], [ttmp[2]])
                                b.tt("dve", tmp[3][:M, :], ps[pA][:M, :], sinR, ALU.mult, [tps[pA], trtR], [ttmp[3]])
                                o1, o2 = obc[0] % 4, (obc[0] + 1) % 4
                                obc[0] += 2
                                b.tt("dve", ob[o1][:M, :], tmp[0][:M, :], tmp[1][:M, :], ALU.subtract, [ttmp[0], ttmp[1]], [tob[o1]])
                                b.tt("dve", ob[o2][:M, :], tmp[2][:M, :], tmp[3][:M, :], ALU.add, [ttmp[2], ttmp[3]], [tob[o2]])
                                hd = dh // 2
                                for hl in range(nh):
                                    row = zb + (h0 + hl) * dh
                                    b.dma(zF[row:row + hd, t0:t0 + TT], ob[o1][hl * hd:(hl + 1) * hd, :], r=[tob[o1]])
                                    b.dma(zF[row + hd:row + dh, t0:t0 + TT], ob[o2][hl * hd:(hl + 1) * hd, :], r=[tob[o2]])
                            else:
                                _, col, M, zrow = j
                                pA = psi % 4
                                psi += 1
                                wtile, off = wcols(col, M)
                                wi_ = loaded[lidx(col)]
                                for kc in range(KC):
                                    b.mm(ps[pA][:M, :], wtile[:, kc, off:off + M], hb[:, kc, :], kc == 0, kc == KC - 1,
                                         [twt[wi_], thb[kc]], [tps[pA]])
                                o1 = obc[0] % 4
                                obc[0] += 1
                                b.tt("dve", ob[o1][:M, :], ps[pA][:M, :], rB[:M, :], ALU.mult, [tps[pA], trB], [tob[o1]])
                                b.dma(zF[zrow:zrow + M, t0:t0 + TT], ob[o1][:M, :], r=[tob[o1]])

                        i1 = wcnt[0] % 2
                        wcnt[0] += 1
                        b.dma(wt[i1][:, :, 0:512], Wi[:, vAV:vAV + 512].rearrange("(k p) n -> p k n", p=128), w=[twt[i1]])
                        i2 = wcnt[0] % 2
                        wcnt[0] += 1
                        b.dma(wt[i2][:, :, 0:136], Wi[:, vBV:vBV + 136].rearrange("(k p) n -> p k n", p=128), w=[twt[i2]])
                        for sbk in range(4):
                            vi = sbk % 2
                            pA, pB = psi % 4, (psi + 1) % 4
                            psi += 2
                            for kc in range(KC):
                                b.mm(ps[pA][:, :], hb[:, kc, sbk * 128:(sbk + 1) * 128], wt[i1][:, kc, 0:512], kc == 0, kc == KC - 1,
                                     [twt[i1], thb[kc]], [tps[pA]])
                            for kc in range(KC):
                                b.mm(ps[pB][:, 0:136], hb[:, kc, sbk * 128:(sbk + 1) * 128], wt[i2][:, kc, 0:136], kc == 0, kc == KC - 1,
                                     [twt[i2], thb[kc]], [tps[pB]])
                            b.ts("dve", vo[vi][:, 0:512], ps[pA][:, :], rcol[:, sbk:sbk + 1], ALU.mult, [tps[pA], trcol], [tvo[vi]])
                            b.ts("dve", vo[vi][:, 512:640], ps[pB][:, 0:128], rcol[:, sbk:sbk + 1], ALU.mult, [tps[pB], trcol], [tvo[vi]])
                            b.ts("dve", iwo[vi][:, :], ps[pB][:, 128:136], rcol[:, sbk:sbk + 1], ALU.mult, [tps[pB], trcol], [tiwo[vi]],
                                 s2=(8 ** -0.5) * (64 ** -0.5), op1=ALU.mult)
                            b.dma(vtok[t0 + sbk * 128:t0 + (sbk + 1) * 128, :], vo[vi][:, :], r=[tvo[vi]])
                            b.dma(iwt[t0 + sbk * 128:t0 + (sbk + 1) * 128, :], iwo[vi][:, :], r=[tiwo[vi]])
                S.barrier()
                tzg = toks(NZC)
                tvg = toks(TL // VCH)
                for k_ in range(TL // VCH):
                    b.allgather(vtokg[k_ * 2 * VCH:(k_ + 1) * 2 * VCH, :], vtok[k_ * VCH:(k_ + 1) * VCH, :], w=[tvg[k_]])
                for k_ in range(NZC):
                    rows_k = min(RCH, ZR - k_ * RCH)
                    b.allgather(zFg[k_ * 2 * RCH:k_ * 2 * RCH + 2 * rows_k, :], zF[k_ * RCH:k_ * RCH + rows_k, :], w=[tzg[k_]])


            if stages is None or "2" in stages:
                with ExitStack() as s0:
                    asets = [([sb(s0, "A_k%d_%d" % (j_, i), [65, T], BF16) for i in range(2)],
                              [sb(s0, "A_q%d_%d" % (j_, i), [65, TL], BF16) for i in range(2)],
                              sb(s0, "A_v%d" % j_, [128, 32, 128], BF16), toks(2), toks(2), Tk()) for j_ in range(2)]
                    sqb = sb(s0, "A_sqb", [64, T], BF16)
                    tsqb = Tk()
                    sel = sb(s0, "A_sel", [64, 65], BF16)
                    kmx = sb(s0, "A_kmx", [65, 8], F32)
                    kmax = sb(s0, "A_kmax", [65, 1], F32)
                    tkm = Tk()
                    lmv = sb(s0, "A_lmv", [128, 4, 64], F32)
                    ltmp = sb(s0, "A_ltmp", [128, 64], F32)
                    lsc = sb(s0, "A_lsc", [128, 4], F32)
                    tl = Tk()
                    pT_ = [sb(s0, "A_pT%d" % i, [128, TT], BF16) for i in range(3)]
                    tpT = toks(3)
                    rl = sb(s0, "A_rl", [128, TT], F32)
                    trl = Tk()
                    oc = [sb(s0, "A_oc%d" % i, [128, TT], F32) for i in range(2)]
                    toc = toks(2)
                    dif = sb(s0, "A_dif", [128, TT], F32)
                    dsq = sb(s0, "A_dsq", [128, TT], BF16)
                    dsd = sb(s0, "A_dsd", [128, TT], F32)
                    yo = [sb(s0, "A_yo%d" % i, [128, TT], BF16) for i in range(2)]
                    tdif, tdsq, tdsd = Tk(), Tk(), Tk()
                    tyo = toks(2)
                    bg_attach(s0)
                    bgs["eng"], bgs["qs"] = "dve", "sp"
                    qidxB = sb(s0, "A_qidxB", [128, TL], F32)
                    bm = sb(s0, "A_bm", [128, NTL, 8, TT], BF16)
                    tbm = Tk()
                    b.dma(qidxB[:], qidxB_d[:, :], w=[tbm])
                    for i_ in range(NTL):
                        gmin_ = min(G[0][i_], G[1][i_])
                        for j_ in range(8):
                            b.ts("dve", bm[:, i_, j_, :], qidxB[:, i_ * TT:(i_ + 1) * TT], pio[:, 0:1], ALU.subtract, [tbm, tconst], [tbm],
                                 s2=float((4 * gmin_ + j_) * 128), op1=ALU.is_ge)
                    b.dma(lmv[:], lamv[L], w=[tl])
                    for i in range(2):
                        b.tt("dve", ltmp[:], lmv[:, 2 * i, :], lmv[:, 2 * i + 1, :], ALU.mult, [tl], [tl])
                        b.red(lsc[:, i:i + 1], ltmp[:], ALU.add, [tl], [tl])
                    b.act(lsc[:, 0:2], lsc[:, 0:2], AF.Exp, [tl], [tl])
                    b.tt("dve", lsc[:, 2:3], lsc[:, 1:2], lsc[:, 0:1], ALU.subtract, [tl], [tl])
                    b.ts("dve", lsc[:, 2:3], lsc[:, 2:3], -lam_init, ALU.add, [tl], [tl])
                    b.ts("dve", lsc[:, 3:4], vcol(L, 72), 1.0 - lam_init, ALU.mult, [tl, tconst], [tl])
                    b.memset("dve", sel[:], 0.0, [tl])
                    b.memset("dve", sel[:, 64:65], 1.0, [tl])
                    for j_ in range(2):
                        for c in range(2):
                            b.memset("dve", asets[j_][0][c][64:65, :], -1.0, [asets[j_][3][c]])
                    scale = 64 ** -0.5
                    pcnt = [0]

                    def prologue(h):
                        kA, qA, vA, tkA, tqA, tvA = asets[h % 2]
                        for g_ in range(8):
                            r0 = vg_row(RG[g_], LG[g_] * TT)
                            b.dma(vA[:, 4 * g_:4 * g_ + 4, :],
                                  vtokg[r0:r0 + TT, h * 128:(h + 1) * 128].rearrange("(c p) e -> p c e", p=128),
                                  r=[tvg[(LG[g_] * TT) // VCH]], w=[tvA])
                        for c in range(2):
                            hc = 2 * h + c
                            for g_ in range(8):
                                (zr, zn, zo), = zg_pieces(RG[g_], 512 + hc * 64, 64)
                                b.dma(kA[c][0:64, g_ * TT:(g_ + 1) * TT], zFg[zr:zr + 64, LG[g_] * TT:(LG[g_] + 1) * TT],
                                      r=[tzg[(512 + hc * 64) // RCH]], w=[tkA[c]])
                            b.dma(qA[c][0:64, :], zF[hc * 64:(hc + 1) * 64, :], w=[tqA[c]])
                            b.act(sqb[:], kA[c][0:64, :], AF.Square, [tkA[c]], [tsqb])
                            for t8 in range(8):
                                b.mm(psx[0:65, :], sel[:], sqb[:, t8 * TT:(t8 + 1) * TT], True, True, [tsqb, tl], [tpsx])
                                b.red(kmx[64:65, t8:t8 + 1], psx[64:65, :], ALU.max, [tpsx], [tkm])
                            b.red(kmax[64:65, :], kmx[64:65, :], ALU.max, [tkm], [tkm])
                            b.act(sqb[:, 0:TL], qA[c][0:64, :], AF.Square, [tqA[c]], [tsqb])
                            for t8 in range(NTL):
                                b.mm(psx[0:65, :], sel[:], sqb[:, t8 * TT:(t8 + 1) * TT], True, True, [tsqb, tl], [tpsx])
                                b.act(qA[c][64:65, t8 * TT:(t8 + 1) * TT], psx[64:65, :], AF.Sqrt, [tpsx, tkm], [tqA[c]],
                                      scale=kmax[64:65, 0:1])
                    prologue(0)
                    for h in range(4):
                        kA, qA, vA, tkA, tqA, tvA = asets[h % 2]
                        gmx = [max(G[0][i_], G[1][i_]) for i_ in range(NTL)]
                        gmn = [min(G[0][i_], G[1][i_]) for i_ in range(NTL)]
                        steps = [(qt, c, sc) for qt in range(NTL) for c in range(2) for sc in range(4 * gmx[qt] + 4)]
                        base = pcnt[0]

                        def issueS(k):
                            qt, c, sc = steps[k]
                            pa = (base + k) % 3
                            b.mm(ps[pa][:], kA[c][:, sc * 128:(sc + 1) * 128], qA[c][:, qt * TT:(qt + 1) * TT], True, True,
                                 [tkA[c], tqA[c]], [tps[pa]])
                        issueS(0)
                        issueS(1)
                        for k, (qt, c, sc) in enumerate(steps):
                            q0 = qt * TT
                            po, pl = ps[3 + c], ps[5 + c]
                            nsc = 4 * gmx[qt] + 4
                            pa = (base + k) % 3
                            b.act(pT_[pa][:], ps[pa][:], AF.Exp, [tps[pa]], [tpT[pa]], scale=scale)
                            if sc >= 4 * gmn[qt]:
                                b.tt("dve", pT_[pa][:], pT_[pa][:], bm[:, qt, sc - 4 * gmn[qt], :], ALU.mult, [tpT[pa], tbm], [tpT[pa]])
                            b.mm(po[:], vA[:, sc, :], pT_[pa][:], sc == 0, sc == nsc - 1, [tvA, tpT[pa]], [tps[3 + c]])
                            b.mm(pl[:], ones[:], pT_[pa][:], sc == 0, sc == nsc - 1, [tconst, tpT[pa]], [tps[5 + c]])
                            if k + 2 < len(steps):
                                issueS(k + 2)
                            if k % 6 == 5:
                                bg_pump(1)
                            if k == len(steps) // 3 and h + 1 < 4:
                                prologue(h + 1)
                            if sc == nsc - 1:
                                b.recip(rl[:], pl[:], [tps[5 + c]], [trl])
                                b.tt("dve", oc[c][:], po[:], rl[:], ALU.mult, [tps[3 + c], trl], [toc[c]])
                                if c == 1:
                                    b.stt(dif[:], oc[1][:], lsc[:, 2:3], oc[0][:], ALU.mult, ALU.add, [toc[0], toc[1], tl], [tdif])
                                    b.act(dsq[:], dif[:], AF.Square, [tdif], [tdsq])
                                    b.mm(psx[:], ones[:], dsq[:], True, True, [tconst, tdsq], [tpsx])
                                    b.act(dsd[:], psx[:], AF.Sqrt, [tpsx, tconst], [tdsd], scale=1.0 / 128, bias=epsc[:, 0:1])
                                    b.recip(dsd[:], dsd[:], [tdsd], [tdsd])
                                    yi = qt % 2
                                    b.stt(yo[yi][:], dif[:], lsc[:, 3:4], dsd[:], ALU.mult, ALU.mult, [tdif, tdsd, tl], [tyo[yi]])
                                    b.dma(ycat[h * 128:(h + 1) * 128, q0:q0 + TT], yo[yi][:], r=[tyo[yi]])
                        pcnt[0] = base + len(steps)
                    bg_detach()
                S.barrier()


            if stages is None or "3" in stages:
                with ExitStack() as s0:
                    ik = sb(s0, "B_ik", [64, T], BF16)
                    iw_ = sb(s0, "B_iw", [128, 16, 8], F32)
                    dW = [sb(s0, "B_dW%d" % i, [128, 8, 128], F32) for i in range(2)]
                    tdW = toks(2)
                    pen = sb(s0, "B_pen", [128, 1024], F32)
                    tpen = Tk()
                    bk = sb(s0, "B_bk", [128, T], BF16)
                    bv = sb(s0, "B_bv", [128, 32, 128], BF16)
                    tik, tiw, tbk, tbv = Tk(), Tk(), Tk(), Tk()
                    iq = [sb(s0, "B_iq%d" % i, [64, 8, TT], BF16) for i in range(2)]
                    bq = [sb(s0, "B_bq%d" % i, [128, 4, TT], BF16) for i in range(2)]
                    tiq, tbq = toks(2), toks(2)
                    Sc = [sb(s0, "B_Sc%d" % i, [128, T], F32) for i in range(2)]
                    tSc = toks(2)
                    junk = sb(s0, "B_junk", [128, T], BF16)
                    tjunk = Tk()
                    msk = [sb(s0, "B_msk%d" % i, [128, T], F32) for i in range(2)]
                    tmsk = toks(2)
                    mT = sb(s0, "B_mT", [128, 32, TT], BF16)
                    tmT = Tk()
                    tmp = [sb(s0, "B_tmp%d" % i, [128, TT], F32) for i in range(3)]
                    ttmp = toks(3)
                    sm = [sb(s0, "B_sm%d" % i, [128, 8], F32) for i in range(2)]
                    steps = [sb(s0, "B_steps%d" % i, [128, 16], F32) for i in range(2)]
                    pow2 = sb(s0, "B_pow2", [128, 16], F32)
                    stp2 = [sb(s0, "B_stp2_%d" % i, [128, 16], F32) for i in range(2)]
                    tsm = toks(2)
                    tp2 = Tk()
                    pT_ = [sb(s0, "B_pT%d" % i, [128, TT], BF16) for i in range(3)]
                    tpT = toks(3)
                    rl = sb(s0, "B_rl", [128, TT], F32)
                    trl = Tk()
                    yo = [sb(s0, "B_yo%d" % i, [128, TT], BF16) for i in range(2)]
                    tyo = toks(2)
                    sqk = sb(s0, "B_sqk", [128, TT], BF16)
                    tsqk = Tk()
                    kmx = sb(s0, "B_kmx", [1, 9], F32)
                    tkm = Tk()
                    mrow = [sb(s0, "B_mrow%d" % i, [1, 4, TT], BF16) for i in range(2)]
                    tmrow = toks(2)
                    negone = sb(s0, "B_negone", [1, 128], BF16)
                    tneg = Tk()
                    NIT = 16
                    kio = sb(s0, "B_kio", [128, T], F32)
                    tkio = Tk()
                    b.iota(kio[:], [[1, T]], [], [tkio])
                    for g_ in range(8):
                        cs_ = slice(LG[g_] * TT, (LG[g_] + 1) * TT)
                        (zr, zn, zo), = zg_pieces(RG[g_], 2176, 64)
                        b.dma(ik[:, g_ * TT:(g_ + 1) * TT], zFg[zr:zr + 64, cs_], r=[tzg[2176 // RCH]], w=[tik])
                        (zr, zn, zo), = zg_pieces(RG[g_], 1536, 128)
                        b.dma(bk[:, g_ * TT:(g_ + 1) * TT], zFg[zr:zr + 128, cs_], r=[tzg[1536 // RCH]], w=[tbk])
                        r0 = vg_row(RG[g_], LG[g_] * TT)
                        b.dma(bv[:, 4 * g_:4 * g_ + 4, :], vtokg[r0:r0 + TT, 512:640].rearrange("(c p) e -> p c e", p=128),
                              r=[tvg[(LG[g_] * TT) // VCH]], w=[tbv])
                    b.dma(iw_[:], iwt.rearrange("(c p) h -> p c h", p=128), w=[tiw])
                    for k in range(NIT):
                        b.memset("pool", pow2[:, k:k + 1], 2.0 ** -(k + 1), [tp2])
                    b.memset("pool", negone[:], -1.0, [tneg])
                    for t8 in range(8):
                        b.act(sqk[:], bk[:, t8 * TT:(t8 + 1) * TT], AF.Square, [tbk], [tsqk])
                        b.mm(ps[t8 % 3][0:1, :], ones[:, 0:1], sqk[:], True, True, [tsqk, tconst], [tps[t8 % 3]])
                        b.red(kmx[0:1, t8:t8 + 1], ps[t8 % 3][0:1, :], ALU.max, [tps[t8 % 3]], [tkm])
                    b.red(kmx[0:1, 8:9], kmx[0:1, 0:8], ALU.max, [tkm], [tkm])
                    pcnt = [0]
                    tcnt = [0]
                    hbk = [0]

                    def tile_info(qt):
                        return max(G[0][qt], G[1][qt]), min(G[0][qt], G[1][qt])

                    def gen_scores(qb):
                        qt, ql = qb // 4, qb % 4
                        gmax, gmin = tile_info(qt)
                        bi = qt % 2
                        q0 = qt * TT
                        if ql == 0:
                            b.dma(iq[bi][:], zF[1664:2176, q0:q0 + TT].rearrange("(h d) t -> d h t", d=64), w=[tiq[bi]])
                            b.dma(bq[bi][:], zF[1024:1536, q0:q0 + TT].rearrange("(h d) t -> d h t", d=128), w=[tbq[bi]])
                            for h in range(4):
                                b.act(sqk[:], bq[bi][:, h, :], AF.Square, [tbq[bi]], [tsqk])
                                pa = pcnt[0] % 3
                                pcnt[0] += 1
                                b.mm(ps[pa][0:1, :], ones[:, 0:1], sqk[:], True, True, [tsqk, tconst], [tps[pa]])
                                b.act(mrow[bi][0:1, h, :], ps[pa][0:1, :], AF.Sqrt, [tps[pa], tkm], [tmrow[bi]], scale=kmx[0:1, 8:9])
                        n = (4 * gmax + ql + 1) * 128
                        si = qb % 2
                        sc_, tsc_ = Sc[si], tSc[si]
                        for h in range(8):
                            b.act(dW[si][:, h, :], identf[:], AF.Copy, [tconst, tiw], [tdW[si]], scale=iw_[:, qb, h:h + 1])
                        ssteps = [(st_, h) for st_ in range((n + 511) // 512) for h in range(8)]
                        sbase = pcnt[0]

                        def issueI(k):
                            st_, h = ssteps[k]
                            c0 = st_ * 512
                            ncol = min(512, n - c0)
                            pa = (sbase + k) % 3
                            b.mm(ps[pa][:, 0:ncol], iq[bi][:, h, ql * 128:(ql + 1) * 128], ik[:, c0:c0 + ncol], True, True,
                                 [tiq[bi], tik], [tps[pa]])
                        issueI(0)
                        issueI(1)
                        for k, (st_, h) in enumerate(ssteps):
                            c0 = st_ * 512
                            ncol = min(512, n - c0)
                            pa = (sbase + k) % 3
                            ti = (sbase + k) % 3
                            acc, tacc = (ps[3], tps[3]) if st_ % 2 == 0 else (ps[4], tps[4])
                            b.act(tmp[ti][:, 0:ncol], ps[pa][:, 0:ncol], AF.Relu, [tps[pa]], [ttmp[ti]])
                            b.mm(acc[:, 0:ncol], dW[si][:, h, :], tmp[ti][:, 0:ncol], h == 0, h == 7, [tdW[si], ttmp[ti]], [tacc])
                            if k + 2 < len(ssteps):
                                issueI(k + 2)
                            if h == 7:
                                b.cp("dve", sc_[:, c0:c0 + ncol], acc[:, 0:ncol], [tacc], [tsc_])
                            yield
                        pcnt[0] = sbase + len(ssteps)

                    def gen_select(qb):
                        qt, ql = qb // 4, qb % 4
                        gmax, gmin = tile_info(qt)
                        n = (4 * gmax + ql + 1) * 128
                        k0 = 4 * gmin * 128
                        si = qb % 2
                        sc_, tsc_ = Sc[si], tSc[si]
                        sm_, tsm_ = sm[si], tsm[si]
                        stp = steps[si]
                        b.red(sm_[:, 0:1], sc_[:, 0:n], ALU.min, [tsc_], [tsm_])
                        b.ts("dve", pen[:, 0:n - k0], kio[:, k0:n], qcol[:, qb:qb + 1], ALU.is_gt, [tconst, tkio, tsm_], [tpen],
                             s2=NEG, op1=ALU.mult)
                        b.tt("pool", sc_[:, k0:n], sc_[:, k0:n], pen[:, 0:n - k0], ALU.add, [tsc_, tpen], [tsc_])
                        yield
                        b.red(sm_[:, 1:2], sc_[:, 0:n], ALU.max, [tsc_], [tsm_])
                        b.tt("dve", sm_[:, 2:3], sm_[:, 1:2], sm_[:, 0:1], ALU.subtract, [tsm_], [tsm_])
                        b.ts("dve", stp[:, :], pow2[:, :], sm_[:, 2:3], ALU.mult, [tsm_, tp2], [tsm_], s2=0.5, op1=ALU.mult)
                        b.ts("dve", stp2[si][:, :], pow2[:, :], sm_[:, 2:3], ALU.mult, [tsm_, tp2], [tsm_])
                        b.stt(sm_[:, 4:5], sm_[:, 2:3], 0.5, sm_[:, 0:1], ALU.mult, ALU.add, [tsm_], [tsm_])
                        yield
                        for k in range(NIT):
                            b.ts("dve", junk[:, 0:n], sc_[:, 0:n], sm_[:, 4:5], ALU.is_ge, [tsc_, tsm_], [tjunk, tsm_],
                                 s2=0.0, op1=ALU.add, accum=sm_[:, 5:6])
                            yield
                            b.stt(sm_[:, 6:7], sm_[:, 5:6], krow[:, qb:qb + 1], stp2[si][:, k:k + 1], ALU.is_ge, ALU.mult, [tsm_, tconst], [tsm_])
                            yield
                            b.stt(sm_[:, 4:5], sm_[:, 6:7], stp[:, k:k + 1], sm_[:, 4:5], ALU.subtract, ALU.add, [tsm_], [tsm_])
                            yield
                        b.tt("dve", sm_[:, 3:4], sm_[:, 4:5], stp2[si][:, NIT - 1:NIT], ALU.subtract, [tsm_], [tsm_])
                        mk, tmk = msk[si], tmsk[si]
                        b.ts("dve", mk[:, 0:n], sc_[:, 0:n], sm_[:, 3:4], ALU.is_ge, [tsc_, tsm_], [tmk])
                        if ql < 3:
                            b.memset("pool", mT[:, n // 128:4 * gmax + 4, ql * 128:(ql + 1) * 128], 0.0, [tmT])
                        for j0 in range(0, n // 128, 4):
                            nj = min(4, n // 128 - j0)
                            tb_, ttb_ = (ps[5], tps[5]) if hbk[0] % 2 == 0 else (psx, tpsx)
                            hbk[0] += 1
                            for j in range(nj):
                                b.tr(tb_[:, j * 128:(j + 1) * 128], mk[:, (j0 + j) * 128:(j0 + j + 1) * 128], identf[:],
                                     [tmk, tconst], [ttb_])
                            for j in range(nj):
                                b.cp(("dve", "act")[j % 2], mT[:, j0 + j, ql * 128:(ql + 1) * 128], tb_[:, j * 128:(j + 1) * 128],
                                     [ttb_], [tmT])
                            yield

                    def attention(qt):
                        gmax, gmin = tile_info(qt)
                        bi = qt % 2
                        q0 = qt * TT
                        nsc = 4 * gmax + 4
                        psteps = [(h, sc) for h in range(4) for sc in range(nsc)]
                        base = pcnt[0]

                        def issueS(k):
                            h, sc = psteps[k]
                            pa = (base + k) % 3
                            b.mm(ps[pa][:], bk[:, sc * 128:(sc + 1) * 128], bq[bi][:, h, :], True, False, [tbk, tbq[bi]], [tps[pa]])
                            b.mm(ps[pa][:], negone[0:1, :], mrow[bi][0:1, h, :], False, True, [tneg, tmrow[bi]], [tps[pa]])
                        issueS(0)
                        issueS(1)
                        for k, (h, sc) in enumerate(psteps):
                            po, pl = (ps[5], ps[6]) if h % 2 == 0 else (psx, ps[6])
                            po, tpo = (ps[5], tps[5]) if h % 2 == 0 else (psx, tpsx)
                            pl, tpl = ps[6], tps[6]
                            pa = (base + k) % 3
                            b.act(pT_[pa][:], ps[pa][:], AF.Exp, [tps[pa]], [tpT[pa]], scale=128 ** -0.5)
                            b.tt("dve", pT_[pa][:], pT_[pa][:], mT[:, sc, :], ALU.mult, [tpT[pa], tmT], [tpT[pa]])
                            b.mm(po[:], bv[:, sc, :], pT_[pa][:], sc == 0, sc == nsc - 1, [tbv, tpT[pa]], [tpo])
                            b.mm(pl[:], ones[:], pT_[pa][:], sc == 0, sc == nsc - 1, [tconst, tpT[pa]], [tpl])
                            if k + 2 < len(psteps):
                                issueS(k + 2)
                            if sc == nsc - 1:
                                b.recip(rl[:], pl[:], [tpl], [trl])
                                yi = h % 2
                                b.tt("dve", yo[yi][:], po[:], rl[:], ALU.mult, [tpo, trl], [tyo[yi]])
                                b.dma(ycat[512 + h * 128:512 + (h + 1) * 128, q0:q0 + TT], yo[yi][:], r=[tyo[yi]])
                        pcnt[0] = base + len(psteps)

                    NQB = 4 * NTL
                    for _ in gen_scores(0):
                        pass
                    for qb in range(NQB):
                        gsel = gen_select(qb)
                        gsc = gen_scores(qb + 1) if qb + 1 < NQB else iter(())
                        nsteps_sc = 8 * ((( 4 * tile_info((qb + 1) // 4)[0] + (qb + 1) % 4 + 1) * 128 + 511) // 512) if qb + 1 < NQB else 0
                        per = max(1, -(-nsteps_sc // (4 * NIT)))
                        done_sc = False
                        for _ in gsel:
                            for _i in range(per):
                                if next(gsc, None) is None and True:
                                    pass
                        for _ in gsc:
                            pass
                        if qb % 4 == 3:
                            attention(qb // 4)
                S.barrier()


            if stages is None or "4" in stages:
                with ExitStack() as s0:
                    BreT = sb(s0, "C_BreT", [128, 16, 128], BF16)
                    BimT = sb(s0, "C_BimT", [128, 16, 128], BF16)
                    CreT = sb(s0, "C_CreT", [128, 16, 128], BF16)
                    nCreT = sb(s0, "C_nCreT", [128, 16, 128], BF16)
                    nCimT = sb(s0, "C_nCimT", [128, 16, 128], BF16)
                    tW = Tk()
                    mag2 = sb(s0, "C_mag2", [128, 16], F32)
                    thn2 = sb(s0, "C_thn2", [128, 16], F32)
                    cr2 = sb(s0, "C_cr2", [128, 16], F32)
                    sr2 = sb(s0, "C_sr2", [128, 16], F32)
                    nsr2 = sb(s0, "C_nsr2", [128, 16], F32)
                    tP2 = Tk()
                    iof = sb(s0, "C_iof", [128, TT], F32)
                    tio = Tk()
                    with ExitStack() as s1:
                        pr = [sb(s1, "C_pr%d" % i, [128, 256], F32) for i in range(14)]
                        lr, li, ls, Br, Bi, t1_, t2_, sn, cs, mg, fre, fim, bbr, bbi = pr
                        tq = Tk()
                        for i, tl_ in enumerate((lr, li, ls)):
                            b.dma(tl_[:], ssmL1[L, i].rearrange("p o q -> p (o q)"), w=[tq])
                        b.dma(Br[:], ssmB[L, 0].rearrange("p o q -> p (o q)"), w=[tq])
                        b.dma(Bi[:], ssmB[L, 1].rearrange("p o q -> p (o q)"), w=[tq])
                        b.act(ls[:], ls[:], AF.Exp, [tq], [tq])
                        b.tt("dve", t1_[:], li[:], ls[:], ALU.mult, [tq], [tq])
                        b.ts("dve", t1_[:], t1_[:], 1.0 / (2 * math.pi), ALU.mult, [tq], [tq])
                        sincos(s1, t1_[:], 128, 256, sn[:], cs[:], [tq], [tq], "C_sc1")
                        b.tt("dve", t2_[:], lr[:], ls[:], ALU.mult, [tq], [tq])
                        b.act(mg[:], t2_[:], AF.Exp, [tq], [tq])
                        b.tt("dve", cs[:], cs[:], mg[:], ALU.mult, [tq], [tq])
                        b.tt("dve", sn[:], sn[:], mg[:], ALU.mult, [tq], [tq])
                        b.ts("dve", cs[:], cs[:], -1.0, ALU.add, [tq], [tq])
                        b.tt("dve", t1_[:], lr[:], lr[:], ALU.mult, [tq], [tq])
                        b.tt("dve", t2_[:], li[:], li[:], ALU.mult, [tq], [tq])
                        b.tt("dve", t1_[:], t1_[:], t2_[:], ALU.add, [tq], [tq])
                        b.recip(t1_[:], t1_[:], [tq], [tq])
                        b.tt("dve", fre[:], cs[:], lr[:], ALU.mult, [tq], [tq])
                        b.tt("dve", t2_[:], sn[:], li[:], ALU.mult, [tq], [tq])
                        b.tt("dve", fre[:], fre[:], t2_[:], ALU.add, [tq], [tq])
                        b.tt("dve", fre[:], fre[:], t1_[:], ALU.mult, [tq], [tq])
                        b.tt("dve", fim[:], sn[:], lr[:], ALU.mult, [tq], [tq])
                        b.tt("dve", t2_[:], cs[:], li[:], ALU.mult, [tq], [tq])
                        b.tt("dve", fim[:], fim[:], t2_[:], ALU.subtract, [tq], [tq])
                        b.tt("dve", fim[:], fim[:], t1_[:], ALU.mult, [tq], [tq])
                        b.tt("dve", bbr[:], fre[:], Br[:], ALU.mult, [tq], [tq])
                        b.tt("dve", t2_[:], fim[:], Bi[:], ALU.mult, [tq], [tq])
                        b.tt("dve", bbr[:], bbr[:], t2_[:], ALU.subtract, [tq], [tq])
                        b.tt("dve", bbi[:], fre[:], Bi[:], ALU.mult, [tq], [tq])
                        b.tt("dve", t2_[:], fim[:], Br[:], ALU.mult, [tq], [tq])
                        b.tt("dve", bbi[:], bbi[:], t2_[:], ALU.add, [tq], [tq])
                        m8 = sb(s1, "C_m8", [128, 8], F32)
                        hm = sb(s1, "C_hm", [128, 2], F32)
                        b.memset("pool", m8[:], 1.0, [tq])
                        b.memset("pool", hm[:], 1.0, [tq])
                        b.asel(m8[:], m8[:], [[-16, 8]], ALU.is_ge, 0.0, 0, 1, [], [tq])
                        b.asel(m8[:], m8[:], [[16, 8]], ALU.is_ge, 0.0, 15, -1, [], [tq])
                        b.asel(hm[:], hm[:], [[-64, 2]], ALU.is_ge, 0.0, 0, 1, [], [tq])
                        b.asel(hm[:], hm[:], [[64, 2]], ALU.is_ge, 0.0, 63, -1, [], [tq])
                        for jj in range(4):
                            for gl in range(2):
                                k0 = 2 * jj + gl
                                for (dst, src_) in ((BreT, bbr), (BimT, bbi)):
                                    b.ts("dve", dst[:, jj:16:4, gl * 64:(gl + 1) * 64], src_[:, :].rearrange("p (o q) -> p o q", q=64),
                                         m8[:, k0:k0 + 1], ALU.mult, [tq], [tW])
                        Cst = [sb(s1, "C_Cst%d" % i, [128, 16, 16], F32) for i in range(2)]
                        b.dma(Cst[0][:], ssmC[L, 0], w=[tq])
                        b.dma(Cst[1][:], ssmC[L, 1], w=[tq])
                        for dst in (CreT, nCreT, nCimT):
                            b.memset("pool", dst[:], 0.0, [tW])
                        for jj in range(4):
                            for gl in range(2):
                                k0 = 2 * jj + gl
                                for (dst, src_, sgn) in ((CreT, Cst[0], 1.0), (nCreT, Cst[0], -1.0), (nCimT, Cst[1], -1.0)):
                                    b.ts("dve", dst[:, jj:16:4, k0 * 16:(k0 + 1) * 16], src_[:, jj:16:4, :], hm[:, gl:gl + 1], ALU.mult,
                                         [tq], [tW], s2=sgn, op1=ALU.mult)
                        p2 = [sb(s1, "C_p2%d" % i, [128, 16], F32) for i in range(4)]
                        for i in range(3):
                            b.dma(p2[i][:], ssmL2[L, i], w=[tq])
                        b.act(p2[2][:], p2[2][:], AF.Exp, [tq], [tq])
                        b.tt("dve", p2[3][:], p2[0][:], p2[2][:], ALU.mult, [tq], [tq])
                        b.act(mag2[:], p2[3][:], AF.Exp, [tq], [tP2])
                        b.tt("dve", thn2[:], p2[1][:], p2[2][:], ALU.mult, [tq], [tP2])
                        b.ts("dve", thn2[:], thn2[:], 1.0 / (2 * math.pi), ALU.mult, [tP2], [tP2])
                        b.ts("dve", p2[3][:], thn2[:], float(TT), ALU.mult, [tP2, tq], [tq])
                        sincos(s1, p2[3][:], 128, 16, sr2[:], cr2[:], [tq], [tP2], "C_sc2")
                        b.ts("dve", nsr2[:], sr2[:], -1.0, ALU.mult, [tP2], [tP2])
                        ioi = sb(s1, "C_ioi", [128, TT], I32)
                        b.iota(ioi[:], [[1, TT]], [], [tio])
                        b.cp("pool", iof[:], ioi[:], [tio], [tio])
                    S.barrier()
                    bg_attach(s0)
                    bgs["eng"], bgs["qs"] = "act", "act"
                    uo = [sb(s0, "C_uo%d" % i, [128, T], BF16) for i in range(2)]
                    tuo = toks(2)
                    tabs = [[sb(s0, "C_tab%d_%d" % (i, k), [128, 2, TT], F32) for k in range(4)] for i in range(2)]
                    ttab = [toks(4) for _ in range(2)]
                    ub = sb(s0, "C_ub", [128, TT], F32)
                    tub = Tk()
                    aa = [[sb(s0, "C_a%d_%d" % (i, k), [128, TT], F32) for k in range(4)] for i in range(3)]
                    taa = [toks(4) for _ in range(3)]
                    bp = [[sb(s0, "C_bp%d_%d" % (i, k), [128, TT], F32) for k in range(2)] for i in range(3)]
                    tbp = [toks(2) for _ in range(3)]
                    ww = [[sb(s0, "C_w%d_%d" % (i, k), [128, TT], F32) for k in range(2)] for i in range(3)]
                    tww = [toks(2) for _ in range(3)]
                    pp = [[sb(s0, "C_p%d_%d" % (i, k), [128, TT], BF16) for k in range(4)] for i in range(3)]
                    tpp = [toks(4) for _ in range(3)]
                    car = sb(s0, "C_car", [128, 2, 16], F32)
                    tcar = toks(16)
                    ctmp = sb(s0, "C_ctmp", [128, 4, 16], F32)
                    ucand = [sb(s0, "C_ucand%d" % i, [128, T], BF16) for i in range(2)]
                    tucand = toks(2)
                    yo = [sb(s0, "C_yo%d" % i, [128, TT], F32) for i in range(2)]
                    tyo = toks(2)
                    b.memset("pool", car[:], 0.0, tcar)
                    it = [0]
                    for o in range(4):
                        ui = o % 2
                        for cand in range(2):
                            row = 2240 + (4 * cand + o) * 128
                            for g_ in range(8):
                                for (zr, zn, zo) in zg_pieces(RG[g_], row, 128):
                                    b.dma(ucand[cand][zo:zo + zn, g_ * TT:(g_ + 1) * TT],
                                          zFg[zr:zr + zn, LG[g_] * TT:(LG[g_] + 1) * TT], r=tzg, w=[tucand[cand]])
                        b.ts("dve", uo[ui][:], ucand[0][:], selv[:, 0:1], ALU.mult, [tucand[0], tconst], [tuo[ui]])
                        b.stt(uo[ui][:], ucand[1][:], selv[:, 1:2], uo[ui][:], ALU.mult, ALU.add, [tucand[1], tconst, tuo[ui]], [tuo[ui]])
                        for jl in range(4):
                            j = 4 * o + jl
                            b.ts("pool", ub[:], iof[:], thn2[:, j:j + 1], ALU.mult, [tio, tP2], [tub])
                            sincos(s0, ub[:], 128, TT, tabs[ui][jl][:, 1, :], tabs[ui][jl][:, 0, :], [tub], [ttab[ui][jl]],
                                   "C_sc3", eng="pool")
                        psteps4 = [(tt_, jl) for tt_ in range(NT) for jl in range(4)]

                        def issueB(k):
                            tt_, jl = psteps4[k]
                            bi = k % 3
                            j = 4 * o + jl
                            b.mm(ps[2 * bi][:], BreT[:, j, :], uo[ui][:, tt_ * TT:(tt_ + 1) * TT], True, True, [tW, tuo[ui]], [tps[2 * bi]])
                            b.mm(ps[2 * bi + 1][:], BimT[:, j, :], uo[ui][:, tt_ * TT:(tt_ + 1) * TT], True, True, [tW, tuo[ui]], [tps[2 * bi + 1]])
                        issueB(0)
                        issueB(1)
                        for k4, (tt_, jl) in enumerate(psteps4):
                            t0 = tt_ * TT
                            ypo, typo = (ps[6], tps[6]) if tt_ % 2 == 0 else (psx, tpsx)
                            j = 4 * o + jl
                            bi = k4 % 3
                            pA, pB = ps[2 * bi], ps[2 * bi + 1]
                            tA, tB = tps[2 * bi], tps[2 * bi + 1]
                            cosT, sinT = tabs[ui][jl][:, 0, :], tabs[ui][jl][:, 1, :]
                            ttb = ttab[ui][jl]
                            a_, ta_ = aa[bi], taa[bi]
                            b.cp("act", a_[0][:], pA[:], [tA], [ta_[0]])
                            b.cp("act", a_[1][:], pB[:], [tB], [ta_[1]])
                            b.tt("dve", a_[3][:], a_[0][:], sinT, ALU.mult, [ta_[0], ttb], [ta_[3]])
                            b.tt("dve", a_[2][:], a_[1][:], cosT, ALU.mult, [ta_[1], ttb], [ta_[2]])
                            b.tt("dve", a_[0][:], a_[0][:], cosT, ALU.mult, [ta_[0], ttb], [ta_[0]])
                            b.tt("dve", a_[1][:], a_[1][:], sinT, ALU.mult, [ta_[1], ttb], [ta_[1]])
                            if k4 + 2 < len(psteps4):
                                issueB(k4 + 2)
                            b.tt("pool", bp[bi][0][:], a_[0][:], a_[1][:], ALU.add, [ta_[0], ta_[1]], [tbp[bi][0]])
                            b.tt("dve", bp[bi][1][:], a_[2][:], a_[3][:], ALU.subtract, [ta_[2], ta_[3]], [tbp[bi][1]])
                            for ri in range(2):
                                b.scan(ww[bi][ri][:], mag2[:, j:j + 1].to_broadcast([128, TT]), bp[bi][ri][:], car[:, ri, j:j + 1],
                                       [tP2, tbp[bi][ri], tcar[j]], [tww[bi][ri]])
                            wr_l, wi_l = ww[bi][0][:, TT - 1:TT], ww[bi][1][:, TT - 1:TT]
                            c4 = [ctmp[:, k, j:j + 1] for k in range(4)]
                            crj, srj, nsrj = cr2[:, j:j + 1], sr2[:, j:j + 1], nsr2[:, j:j + 1]
                            tc = tcar[j]
                            b.act(c4[0], wr_l, AF.Copy, [tww[bi][0], tP2], [tc], scale=crj)
                            b.act(car[:, 0, j:j + 1], wi_l, AF.Identity, [tww[bi][1], tP2, tc], [tc], scale=nsrj, bias=c4[0])
                            b.act(c4[1], wr_l, AF.Copy, [tww[bi][0], tP2], [tc], scale=srj)
                            b.act(car[:, 1, j:j + 1], wi_l, AF.Identity, [tww[bi][1], tP2, tc], [tc], scale=crj, bias=c4[1])
                            p_, tp_ = pp[bi], tpp[bi]
                            b.tt("pool", p_[0][:], ww[bi][0][:], cosT, ALU.mult, [tww[bi][0], ttb], [tp_[0]])
                            b.tt("dve", p_[1][:], ww[bi][1][:], sinT, ALU.mult, [tww[bi][1], ttb], [tp_[1]])
                            b.tt("dve", p_[2][:], ww[bi][1][:], cosT, ALU.mult, [tww[bi][1], ttb], [tp_[2]])
                            b.tt("dve", p_[3][:], ww[bi][0][:], sinT, ALU.mult, [tww[bi][0], ttb], [tp_[3]])
                            for k, lh in enumerate((CreT, nCreT, nCimT, nCimT)):
                                b.mm(ypo[:], lh[:, j, :], p_[k][:], jl == 0 and k == 0, jl == 3 and k == 3,
                                     [tW, tp_[k]], [typo])
                            bg_pump(1)
                            if jl == 3:
                                yi = tt_ % 2
                                b.stt(yo[yi][:], uo[ui][:, t0:t0 + TT], vcol(L, 64 + o), ypo[:], ALU.mult, ALU.add,
                                      [tuo[ui], typo, tconst], [tyo[yi]])
                                b.dma(yprel[o * 128:(o + 1) * 128, t0:t0 + TT], yo[yi][:], r=[tyo[yi]])
                    bg_detach(all_=True)
                S.barrier()
                for o_ in range(4):
                    b.allgather(ypre[o_ * 256:(o_ + 1) * 256, :], yprel[o_ * 128:(o_ + 1) * 128, :])
                S.barrier()

            if stages is None or "5a" in stages:
                with ExitStack() as s0:
                    wl = sb(s0, "g_wl", [128, 8, 1024], BF16)
                    twl = Tk()
                    yp = [sb(s0, "g_yp%d" % i, [128, 8, TT], F32) for i in range(2)]
                    typ = [toks(8) for _ in range(2)]
                    yg = sb(s0, "g_yg", [128, 8, TT], BF16)
                    tyg = toks(8)
                    tmp = [sb(s0, "g_tmp%d" % i, [128, TT], F32) for i in range(4)]
                    ttmp = toks(4)
                    sg = [sb(s0, "g_sg%d" % i, [128, TT], BF16) for i in range(2)]
                    tsg = toks(2)
                    yo = [sb(s0, "g_yo%d" % i, [128, TT], BF16) for i in range(2)]
                    tyo = toks(2)
                    b.dma(wl[:], Wl[:, :].rearrange("(k p) n -> p k n", p=128), w=[twl])
                    yq = [sb(s0, "g_yq%d" % i, [128, 8, TT], F32) for i in range(2)]
                    tyq = toks(2)
                    for tt_ in range(NTL):
                        t0 = tt_ * TT
                        bi = tt_ % 2
                        for cand in range(2):
                            g0_ = G[cand][tt_] * TT
                            for c in range(8):
                                ro = (c % 4) * 256 + (c // 4) * 128
                                b.dma(yq[cand][:, c, :], ypre[ro:ro + 128, g0_:g0_ + TT], w=[tyq[cand]])
                        for c in range(8):
                            b.ts("dve", yp[bi][:, c, :], yq[0][:, c, :], selv[:, 0:1], ALU.mult, [tyq[0], tconst], [typ[bi][c]])
                            b.stt(yp[bi][:, c, :], yq[1][:, c, :], selv[:, 1:2], yp[bi][:, c, :], ALU.mult, ALU.add,
                                  [tyq[1], tconst, typ[bi][c]], [typ[bi][c]])
                        for c in range(8):
                            a, a2 = (2 * c) % 4, (2 * c + 1) % 4
                            b.act(tmp[a][:], yp[bi][:, c, :], AF.Square, [typ[bi][c]], [ttmp[a]])
                            b.ts("dve", tmp[a][:], tmp[a][:], 0.044715, ALU.mult, [ttmp[a]], [ttmp[a]], s2=1.0, op1=ALU.add)
                            b.tt(("dve", "pool")[c % 2], tmp[a][:], tmp[a][:], yp[bi][:, c, :], ALU.mult, [ttmp[a], typ[bi][c]], [ttmp[a]])
                            b.act(tmp[a2][:], tmp[a][:], AF.Sigmoid, [ttmp[a]], [ttmp[a2]], scale=1.5957691216057308)
                            b.tt("dve", yg[:, c, :], yp[bi][:, c, :], tmp[a2][:], ALU.mult, [typ[bi][c], ttmp[a2]], [tyg[c]])
                        for m in range(8):
                            pi_ = m % 4
                            for c in range(8):
                                b.mm(ps[pi_][:], wl[:, c, m * 128:(m + 1) * 128], yg[:, c, :], c == 0, c == 7,
                                     [twl, tyg[c]], [tps[pi_]])
                            si = m % 2
                            b.act(sg[si][:], ps[pi_][:], AF.Sigmoid, [tps[pi_]], [tsg[si]])
                            b.tt(("dve", "pool")[m % 2], yo[si][:], yg[:, m, :], sg[si][:], ALU.mult, [tyg[m], tsg[si]], [tyo[si]])
                            b.dma(ycat[1024 + m * 128:1024 + (m + 1) * 128, t0:t0 + TT], yo[si][:], r=[tyo[si]])
                S.barrier()

            if stages is None or "5b" in stages:
                last = (L == n_layers - 1)
                src_h = xT if L == 0 else hT
                with ExitStack() as s0:
                    ht = sb(s0, "d_ht", [128, KC, TT], F32)
                    hb = sb(s0, "d_hb", [128, KC, TT], BF16)
                    yc = hb
                    hid = sb(s0, "d_hid", [128, 64, TT], BF16)
                    tht, thb, thid = toks(KC), toks(KC), toks(64)
                    NWB = 3
                    wt = [sb(s0, "d_wt%d" % i, [128, 8192], BF16) for i in range(NWB)]
                    twt = toks(NWB)
                    wp = sb(s0, "d_wp", [128, 2, D], BF16)
                    twp = Tk()
                    pf_ = sb(s0, "d_pf", [128, 2, TT], F32)
                    pb_ = sb(s0, "d_pb", [128, 2, TT], BF16)
                    tpf, tpb = Tk(), Tk()
                    tmp = [sb(s0, "d_tmp%d" % i, [128, TT], F32) for i in range(4)]
                    ttmp = toks(4)
                    tmpb = [sb(s0, "d_tmpb%d" % i, [128, TT], BF16) for i in range(4)]
                    ttmpb = toks(4)
                    r2 = sb(s0, "d_r2", [128, TT], F32)
                    tr2 = Tk()
                    rs = rstd_from(s0, "d_rs")
                    b.dma(wp[:], Wp[:, :].rearrange("(k p) n -> p k n", p=128), w=[twp])
                    wcnt = [0]
                    tmpc = [0]

                    for tt_ in range(NTL):
                        t0 = tt_ * TT
                        pieces = []
                        for c in range(4):
                            pieces.append(("o", Wo[:, c * 512:(c + 1) * 512].rearrange("(k p) n -> p k n", p=128), 16, 512, c))
                        for c in range(16):
                            pieces.append(("u", Wu[:, c * 512:(c + 1) * 512].rearrange("(k p) n -> p k n", p=128), 16, 512, c))
                        for c in range(8):
                            for hf in range(2):
                                pieces.append(("d", Wd[hf * 4096:(hf + 1) * 4096, c * 256:(c + 1) * 256].rearrange("(k p) n -> p k n", p=128), 32, 256, (c, hf)))
                        for c in range(4):
                            pieces.append(("g", Wg[:, c * 512:(c + 1) * 512].rearrange("(k p) n -> p k n", p=128), 16, 512, c))
                        slots = {}

                        def issue(pi_):
                            kind, src, k_, n_, _ = pieces[pi_]
                            i = wcnt[0] % NWB
                            wcnt[0] += 1
                            b.dma(wt[i][:, 0:k_ * n_].rearrange("p (k n) -> p k n", k=k_), src, w=[twt[i]])
                            slots[pi_] = i

                        for g4 in range(4):
                            b.dma(ht[:, 4 * g4:4 * g4 + 4, :],
                                  src_h[g4 * 512:(g4 + 1) * 512, t0:t0 + TT].rearrange("(k p) t -> p k t", p=128),
                                  w=tht[4 * g4:4 * g4 + 4])
                            b.dma(yc[:, 4 * g4:4 * g4 + 4, :],
                                  ycat[g4 * 512:(g4 + 1) * 512, t0:t0 + TT].rearrange("(k p) t -> p k t", p=128),
                                  w=thb[4 * g4:4 * g4 + 4])
                        b.dma(pf_[:], pT[L, :, t0:t0 + TT].rearrange("(k p) t -> p k t", p=128), w=[tpf])
                        issue(0)
                        issue(1)
                        b.cp("pool", pb_[:], pf_[:], [tpf], [tpb])
                        rB = trB = None
                        psi = 0
                        for pi_, (kind, src, k_, n_, cinfo) in enumerate(pieces):
                            if pi_ + 2 < len(pieces):
                                issue(pi_ + 2)
                            wi_ = slots[pi_]
                            wv = wt[wi_][:, 0:k_ * n_].rearrange("p (k n) -> p k n", k=k_)
                            if kind == "o":
                                for mb in range(4):
                                    m = cinfo * 4 + mb
                                    pa = psi % 5
                                    psi += 1
                                    for kc in range(KC):
                                        b.mm(ps[pa][:], wv[:, kc, mb * 128:(mb + 1) * 128], yc[:, kc, :], kc == 0, kc == KC - 1,
                                             [twt[wi_], thb[kc]], [tps[pa]])
                                    b.tt("dve", ht[:, m, :], ht[:, m, :], ps[pa][:], ALU.add, [tht[m], tps[pa]], [tht[m]])
                                if cinfo == 3:
                                    rB, trB = rs(ht, tht, hb, thb, 6)
                                    b.tt("pool", r2[:], rB[:], rB[:], ALU.mult, [trB], [tr2])
                            elif kind == "u":
                                for mb in range(4):
                                    f = cinfo * 4 + mb
                                    pa = psi % 5
                                    psi += 1
                                    for kc in range(KC):
                                        b.mm(ps[pa][:], wv[:, kc, mb * 128:(mb + 1) * 128], hb[:, kc, :], kc == 0, kc == KC - 1,
                                             [twt[wi_], thb[kc]], [tps[pa]])
                                    tb_ = f % 4
                                    b.act(tmpb[tb_][:], ps[pa][:], AF.Relu, [tps[pa]], [ttmpb[tb_]])
                                    b.tt(("dve", "pool")[f % 2], hid[:, f, :], tmpb[tb_][:], tmpb[tb_][:], ALU.mult, [ttmpb[tb_]], [thid[f]])
                            elif kind == "d":
                                c, hf = cinfo
                                if hf == 0:
                                    dpa = [psi % 5, (psi + 1) % 5]
                                    psi += 2
                                for mb in range(2):
                                    pa = dpa[mb]
                                    for k in range(32):
                                        f = hf * 32 + k
                                        b.mm(ps[pa][:], wv[:, k, mb * 128:(mb + 1) * 128], hid[:, f, :], f == 0, f == 63,
                                             [twt[wi_], thid[f]], [tps[pa]])
                                    if hf == 1:
                                        m = c * 2 + mb
                                        ti = tmpc[0] % 4
                                        tmpc[0] += 1
                                        b.tt("dve", tmp[ti][:], ps[pa][:], r2[:], ALU.mult, [tps[pa], tr2], [ttmp[ti]])
                                        b.tt("pool", ht[:, m, :], ht[:, m, :], tmp[ti][:], ALU.add, [tht[m], ttmp[ti]], [tht[m]])
                                if c == 7 and hf == 1:
                                    rB, trB = rs(ht, tht, hb, thb, 6)
                            elif kind == "g":
                                for mb in range(4):
                                    m = cinfo * 4 + mb
                                    pa, pb2 = psi % 5, (psi + 1) % 5
                                    psi += 2
                                    for kc in range(KC):
                                        b.mm(ps[pa][:], wv[:, kc, mb * 128:(mb + 1) * 128], hb[:, kc, :], kc == 0, kc == KC - 1,
                                             [twt[wi_], thb[kc]], [tps[pa]])
                                    for k in range(2):
                                        b.mm(ps[pb2][:], wp[:, k, m * 128:(m + 1) * 128], pb_[:, k, :], k == 0, k == 1,
                                             [twp, tpb], [tps[pb2]])
                                    ti, tj = tmpc[0] % 4, (tmpc[0] + 1) % 4
                                    tmpc[0] += 2
                                    b.tt("dve", tmp[ti][:], ps[pa][:], rB[:], ALU.mult, [tps[pa], trB], [ttmp[ti]])
                                    b.act(tmp[tj][:], tmp[ti][:], AF.Sigmoid, [ttmp[ti]], [ttmp[tj]])
                                    b.tt("dve", tmp[ti][:], tmp[tj][:], ps[pb2][:], ALU.mult, [ttmp[tj], tps[pb2]], [ttmp[ti]])
                                    b.tt("pool", ht[:, m, :], ht[:, m, :], tmp[ti][:], ALU.add, [tht[m], ttmp[ti]], [tht[m]])
                        if not last:
                            for g4 in range(4):
                                b.dma(hT[g4 * 512:(g4 + 1) * 512, t0:t0 + TT].rearrange("(k p) t -> p k t", p=128),
                                      ht[:, 4 * g4:4 * g4 + 4, :], r=tht[4 * g4:4 * g4 + 4])
                        else:
                            rB, trB = rs(ht, tht, hb, thb, 6)
                            for m in range(KC):
                                b.stt(ht[:, m, :], ht[:, m, :], vcol(L, 48 + m), rB[:], ALU.mult, ALU.mult,
                                      [tht[m], trB, tconst], [tht[m]])
                            for g4 in range(4):
                                b.dma(outT[g4 * 512:(g4 + 1) * 512, t0:t0 + TT].rearrange("(k p) t -> p k t", p=128),
                                      ht[:, 4 * g4:4 * g4 + 4, :], r=tht[4 * g4:4 * g4 + 4], final=True)
                S.barrier()

        if not S.final:
            fin = sb(st, "fin", [128, 8], F32)
            tf = Tk()
            b.memset("dve", fin[:], 0.0, [tf])
            b.dma(outT[0:128, 0:8], fin[:], r=[tf], final=True)
        S.emit(st)
    return nc


def host_inputs(inputs, bidx, rank):
    f32 = np.float32
    x = np.asarray(inputs["x"], f32)
    p = np.asarray(inputs["p"], f32)
    tok = np.concatenate([np.arange(g * TT, (g + 1) * TT) for g in G[rank]])
    m = {}
    m["xT"] = np.ascontiguousarray(x[bidx][tok].T)
    m["pT"] = np.ascontiguousarray(np.transpose(p[:, bidx][:, tok], (0, 2, 1)))
    pos = np.asarray(inputs["positions"], np.int32)[bidx][tok]
    m["posf"] = np.ascontiguousarray(np.broadcast_to(pos[None, :], (64, TL)))
    m["qidxB"] = np.ascontiguousarray(np.broadcast_to(tok.astype(f32)[None, :], (128, TL)))
    m["qcol"] = np.ascontiguousarray(tok.astype(f32).reshape(16, 128).T)
    sel = np.zeros((128, 2), f32)
    sel[:, rank] = 1.0
    m["selv"] = sel
    fr = np.zeros((64, 2), f32)
    fr[:, 0] = (np.float32(10000.0) ** (-np.arange(0, 128, 2, dtype=f32) / f32(128))).astype(f32)
    fr[:32, 1] = (np.float32(10000.0) ** (-np.arange(0, 64, 2, dtype=f32) / f32(64))).astype(f32)
    m["freqs"] = fr
    m["w_in"] = np.asarray(inputs["w_in"], f32)
    m["w_out"] = np.asarray(inputs["w_out"], f32)
    m["w_up"] = np.asarray(inputs["w_up"], f32)
    m["w_down"] = np.asarray(inputs["w_down"], f32)
    m["w_gate"] = np.asarray(inputs["w_ple_gate"], f32)
    m["w_ple"] = np.asarray(inputs["w_ple_proj"], f32)
    m["w_glu"] = np.asarray(inputs["ssm_w_glu"], f32)
    vecs = np.zeros((2, 128, NVEC), f32)
    for l in range(2):
        vecs[l, :, 0:16] = np.asarray(inputs["norm_mix_g"], f32)[l].reshape(16, 128).T
        vecs[l, :, 16:32] = np.asarray(inputs["norm_mlp_g"], f32)[l].reshape(16, 128).T
        vecs[l, :, 32:48] = np.asarray(inputs["norm_ple_g"], f32)[l].reshape(16, 128).T
        vecs[l, :, 48:64] = np.asarray(inputs["final_g"], f32).reshape(16, 128).T
        vecs[l, :, 64:68] = np.asarray(inputs["ssm_D"], f32)[l].reshape(8, 128)[4 * rank:4 * rank + 4].T
        vecs[l, :, 72] = np.asarray(inputs["diff_subln_g"], f32)[l]
    m["vecs"] = vecs
    lam = np.stack([np.asarray(inputs[k], f32) for k in ("diff_lq1", "diff_lk1", "diff_lq2", "diff_lk2")], axis=1)
    m["lamv"] = np.ascontiguousarray(np.broadcast_to(lam[:, None], (2, 128, 4, 64)))
    gs = slice(32 * rank, 32 * rank + 32)
    lr = np.asarray(inputs["ssm_lambda_re"], f32)[:, gs]
    li = np.asarray(inputs["ssm_lambda_im"], f32)[:, gs]
    ls = np.asarray(inputs["ssm_log_step"], f32)[:, gs]
    ls_full = np.broadcast_to(ls[:, :, None], lr.shape)
    L1 = []
    L2 = []
    for a in (lr, li, ls_full):
        a1 = np.broadcast_to(a.reshape(2, 4, 8, 1, 64), (2, 4, 8, 16, 64))
        L1.append(np.transpose(a1, (0, 2, 3, 1, 4)).reshape(2, 128, 4, 64))
        a2 = a.reshape(2, 16, 2, 64)
        L2.append(np.transpose(a2, (0, 2, 3, 1)).reshape(2, 128, 16))
    m["ssmL1"] = np.ascontiguousarray(np.stack(L1, axis=1))
    m["ssmL2"] = np.ascontiguousarray(np.stack(L2, axis=1))
    Bs = []
    for k in ("ssm_B_re", "ssm_B_im"):
        a = np.asarray(inputs[k], f32)[:, gs]
        a = a.reshape(2, 4, 8, 64, 16)
        Bs.append(np.transpose(a, (0, 2, 4, 1, 3)).reshape(2, 128, 4, 64))
    m["ssmB"] = np.ascontiguousarray(np.stack(Bs, axis=1))
    Cs = []
    for k in ("ssm_C_re", "ssm_C_im"):
        a = np.asarray(inputs[k], f32)[:, gs]
        a = a.reshape(2, 16, 2, 16, 64)
        Cs.append(np.transpose(a, (0, 2, 4, 1, 3)).reshape(2, 128, 16, 16))
    m["ssmC"] = np.ascontiguousarray(np.stack(Cs, axis=1))
    return m


_NC_CACHE = {}


def kernel(**inputs):
    if "nc" not in _NC_CACHE:
        _NC_CACHE["nc"] = build()
    nc = _NC_CACHE["nc"]
    in_maps = [host_inputs(inputs, c // 2, c % 2) for c in range(8)]
    res = run_bass_kernel_spmd(nc, in_maps, core_ids=list(range(8)))
    out = np.zeros((4, T, D), np.float32)
    for c in range(8):
        tok = np.concatenate([np.arange(g * TT, (g + 1) * TT) for g in G[c % 2]])
        out[c // 2][tok] = res.results[c]["outT"].T
    return out
```

```python
import math
import numpy as np
from contextlib import ExitStack
import concourse.bass as bass
import concourse.mybir as mybir
from concourse.bass_utils import run_bass_kernel_spmd

F32 = mybir.dt.float32
BF16 = mybir.dt.bfloat16
I32 = mybir.dt.int32
AF = mybir.ActivationFunctionType
ALU = mybir.AluOpType
AX = mybir.AxisListType

D = 2048
T = 4096
TT = 512
NT = T // TT
TL = 2048
NTL = TL // TT
G = ((0, 3, 4, 7), (1, 2, 5, 6))
RG = (0, 1, 1, 0, 0, 1, 1, 0)
LG = (0, 0, 1, 1, 2, 2, 3, 3)
ZR = 3264
RCH = 512
VCH = 1024
NZC = (ZR + RCH - 1) // RCH
ZGR = 2 * RCH * NZC


def zg_pieces(q, r0, n):
    out = []
    off = 0
    while n > 0:
        k = r0 // RCH
        m = min(n, (k + 1) * RCH - r0)
        rows_k = min(RCH, ZR - k * RCH)
        out.append((k * 2 * RCH + q * rows_k + (r0 - k * RCH), m, off))
        r0 += m
        off += m
        n -= m
    return out


def vg_row(q, r0):
    k = r0 // VCH
    return k * 2 * VCH + q * VCH + (r0 - k * VCH)
PAIRS = [[0, 1], [2, 3], [4, 5], [6, 7]]
KC = D // 128
INW = 3912
DFF = 8192
EPS = 1e-6
AQ, AK, AV, BQ, BK, BV, IQ, IK, IW, CU = 0, 512, 1024, 1536, 2048, 2176, 2304, 2816, 2880, 2888
fAQ, fAK, fBQ, fBK, fIQ, fIK, fCU = 0, 512, 1024, 1536, 1664, 2176, 2240
vAV, vBV, vIW = 3264, 3776, 3904
NVEC = 16 * 4 + 8 + 1
TWO_PI_S = 6.28318
NEG = -3.0e38


class Tk:
    __slots__ = ("w", "r")

    def __init__(self):
        self.w = None
        self.r = []


def toks(n):
    return [Tk() for _ in range(n)]


class Op:
    __slots__ = ("eng", "fn", "dma", "deps", "signal", "sem", "val", "slotwait", "pos", "cc")

    def __init__(self, eng, fn, dma):
        self.pos = 0
        self.cc = False
        self.eng = eng
        self.fn = fn
        self.dma = dma
        self.deps = []
        self.signal = False
        self.sem = None
        self.val = 0
        self.slotwait = None


class Sched:
    COMPUTE = ("pe", "act", "dve", "pool")
    QUEUES = ("pe", "act", "dve", "pool", "sp")
    NSLOT = 8

    def __init__(self, nc):
        self.nc = nc
        self.q = {e: [] for e in self.QUEUES}
        self.final = []
        self.bar = []
        self.bar_epoch = 0
        self.seen = {e: 0 for e in self.QUEUES}
        self.npos = 0
        self.bar_pos = 0

    def barrier(self):
        lasts = []
        for e in self.QUEUES:
            ops = self.q[e]
            comp = [o for o in ops[-64:] if not o.dma]
            if comp:
                lasts.append(comp[-1])
            dm = [o for o in ops if o.dma and not o.cc][-self.NSLOT:]
            lasts.extend(dm)
            lasts.extend(o for o in ops if o.cc and o.pos > self.bar_pos)
        self.bar = lasts
        self.bar_pos = self.npos
        self.bar_epoch += 1

    def op(self, eng, fn, r=(), w=(), dma=False):
        o = Op(eng, fn, dma)
        deps = {}
        for t in r:
            if t.w is not None:
                deps[id(t.w)] = t.w
        for t in w:
            if t.w is not None:
                deps[id(t.w)] = t.w
            for x in t.r:
                deps[id(x)] = x
        if self.seen[eng] < self.bar_epoch:
            self.seen[eng] = self.bar_epoch
            for x in self.bar:
                deps[id(x)] = x
        self.npos += 1
        o.pos = self.npos
        latest = {}
        for d in deps.values():
            if d is o:
                continue
            if (not d.dma) and (not dma) and d.eng == "pe" and eng == "pe":
                continue
            if d.dma:
                o.deps.append(d)
            else:
                if d.eng not in latest or latest[d.eng].pos < d.pos:
                    latest[d.eng] = d
        for d in latest.values():
            o.deps.append(d)
            d.signal = True
        for t in r:
            t.r.append(o)
        for t in w:
            t.w = o
            t.r = []
        self.q[eng].append(o)
        return o

    def emit(self, stack):
        nc = self.nc
        EPOCH = 30000
        nep = {e: sum(1 for o in self.q[e] if (not o.dma) and o.signal) // EPOCH + 1 for e in self.COMPUTE}
        esem = {e: [stack.enter_context(nc.semaphore("s_%s%d" % (e, i))) for i in range(nep[e])] for e in self.COMPUTE}
        dsem = {e: [stack.enter_context(nc.semaphore("d_%s%d" % (e, i))) for i in range(self.NSLOT)]
                for e in self.QUEUES}
        for e in self.QUEUES:
            cnt = 0
            nd = 0
            for o in self.q[e]:
                if o.cc:
                    o.sem = stack.enter_context(nc.semaphore("cc_%d" % o.pos))
                    o.val = 1
                    o.slotwait = (o.sem, 0)
                elif o.dma:
                    s = nd % self.NSLOT
                    o.sem = dsem[e][s]
                    o.val = 16 * (nd // self.NSLOT + 1)
                    o.slotwait = (dsem[e][s], 16 * (nd // self.NSLOT))
                    nd += 1
                elif o.signal:
                    o.sem = esem[e][cnt // EPOCH]
                    o.val = cnt % EPOCH + 1
                    cnt += 1
        block = stack.enter_context(nc.Block())
        handles = {"pe": block.tensor, "act": block.scalar, "dve": block.vector,
                   "pool": block.gpsimd, "sp": block.sync}
        final = self.final

        def make(e):
            ops = self.q[e]

            def body(eng):
                waited = {}

                def wait(sem, val):
                    if val <= 0:
                        return
                    k = id(sem)
                    if waited.get(k, 0) < val:
                        eng.wait_ge(sem, val)
                        waited[k] = val

                for o in ops:
                    for d in o.deps:
                        wait(d.sem, d.val)
                    if o.dma:
                        wait(*o.slotwait)
                    ins = o.fn(eng)
                    if o.cc:
                        ins.then_inc(o.sem)
                    elif o.dma:
                        ins.then_inc(o.sem, 16)
                    elif o.signal:
                        ins.then_inc(o.sem, 1)
                if e == "sp":
                    for o in final:
                        wait(o.sem, o.val)
            return body

        for e in self.QUEUES:
            if self.q[e] or e == "sp":
                handles[e](make(e))


class B:
    def __init__(self, nc, S):
        self.nc = nc
        self.S = S
        self.rr = 0
        self.fillregs = {}

    def mm(self, out, lhsT, rhs, start, stop, r, w):
        return self.S.op("pe", lambda e: e.matmul(out, lhsT=lhsT, rhs=rhs, start=start, stop=stop), r=r, w=w)

    def tr(self, out, in_, ident, r, w):
        return self.S.op("pe", lambda e: e.transpose(out, in_, ident), r=r, w=w)

    def act(self, out, in_, func, r, w, scale=None, bias=None, accum=None):
        kw = {}
        if scale is not None:
            kw["scale"] = scale
        if bias is not None:
            kw["bias"] = bias
        if accum is not None:
            kw["accum_out"] = accum
        return self.S.op("act", lambda e: e.activation(out=out, in_=in_, func=func, **kw), r=r, w=w)

    def tt(self, eng, out, a, b, op, r, w):
        return self.S.op(eng, lambda e: e.tensor_tensor(out=out, in0=a, in1=b, op=op), r=r, w=w)

    def ts(self, eng, out, a, s1, op0, r, w, s2=None, op1=None, accum=None):
        kw = {}
        if op1 is not None:
            kw["op1"] = op1
        if accum is not None:
            kw["accum_out"] = accum
        return self.S.op(eng, lambda e: e.tensor_scalar(out=out, in0=a, scalar1=s1, scalar2=s2, op0=op0, **kw), r=r, w=w)

    def stt(self, out, a, s, b, op0, op1, r, w):
        return self.S.op("dve", lambda e: e.scalar_tensor_tensor(out=out, in0=a, scalar=s, in1=b, op0=op0, op1=op1), r=r, w=w)

    def cp(self, eng, out, in_, r, w):
        if eng == "act":
            return self.S.op("act", lambda e: e.copy(out=out, in_=in_), r=r, w=w)
        return self.S.op(eng, lambda e: e.tensor_copy(out=out, in_=in_), r=r, w=w)

    def memset(self, eng, out, val, w):
        return self.S.op(eng, lambda e: e.memset(out, val), w=w)

    def recip(self, out, in_, r, w):
        return self.S.op("dve", lambda e: e.reciprocal(out=out, in_=in_), r=r, w=w)

    def dma(self, out, in_, r=(), w=(), q="sp", final=False, slow=False):
        kw = {"allow_slow_non_contiguous": True} if slow else {}
        o = self.S.op(q, lambda e: e.dma_start(out=out, in_=in_, **kw), r=r, w=w, dma=True)
        if final:
            self.S.final.append(o)
        return o

    def asel(self, out, in_, pattern, cmp, fill, base, cm, r, w):
        regs = self.fillregs

        def fn(e):
            if fill not in regs:
                regs[fill] = e.to_reg(fill)
            return e.affine_select(out=out, in_=in_, pattern=pattern, compare_op=cmp, fill=regs[fill],
                                   base=base, channel_multiplier=cm)
        return self.S.op("pool", fn, r=r, w=w)

    def red(self, out, in_, op, r, w):
        return self.S.op("dve", lambda e: e.tensor_reduce(out=out, in_=in_, axis=AX.X, op=op), r=r, w=w)

    def iota(self, out, pattern, r, w, cm=0):
        return self.S.op("pool", lambda e: e.iota(out, pattern=pattern, base=0, channel_multiplier=cm,
                                                  allow_small_or_imprecise_dtypes=True), r=r, w=w)

    def scan(self, out, d0, d1, init, r, w):
        return self.S.op("dve", lambda e: e.tensor_tensor_scan(out=out, data0=d0, data1=d1, initial=init,
                                                               op0=ALU.mult, op1=ALU.add), r=r, w=w)

    def allgather(self, out, in_, w=()):
        o = self.S.op("pool", lambda e: e.collective_compute("AllGather", ALU.bypass, replica_groups=PAIRS,
                                                             ins=[in_.opt()], outs=[out.opt()]), r=[], w=list(w), dma=True)
        o.cc = True
        return o

    def cast_eng(self):
        self.rr += 1
        return ("dve", "act", "dve", "act", "dve", "act", "pool")[self.rr % 7]


def build(n_layers=2, debug=None, stages=None, inject=False, dbg3=9, nqt=NT):
    debug = debug or set()
    nc = bass.Bass("TRN2", target_bir_lowering=False)

    def din(name, shape, dt=F32):
        return nc.dram_tensor(name, list(shape), dt, kind="ExternalInput").ap()

    def dscr(name, shape, dt):
        kind = "ExternalOutput" if name in debug else "Internal"
        return nc.dram_tensor(name, list(shape), dt, kind=kind).ap()

    xT = din("xT", [D, TL])
    pT = din("pT", [2, 256, TL])
    posf = din("posf", [64, TL], I32)
    qidxB_d = din("qidxB", [128, TL])
    qcol_d = din("qcol", [128, 16])
    selv_d = din("selv", [128, 2])
    freqs = din("freqs", [64, 2])
    w_in = din("w_in", [2, D, INW])
    w_out = din("w_out", [2, D, D])
    w_up = din("w_up", [2, D, DFF])
    w_down = din("w_down", [2, DFF, D])
    w_gate = din("w_gate", [2, D, D])
    w_ple = din("w_ple", [2, 256, D])
    w_glu = din("w_glu", [2, 1024, 1024])
    vecs = din("vecs", [2, 128, NVEC])
    lamv = din("lamv", [2, 128, 4, 64])
    ssmL1 = din("ssmL1", [2, 3, 128, 4, 64])
    ssmL2 = din("ssmL2", [2, 3, 128, 16])
    ssmB = din("ssmB", [2, 2, 128, 4, 64])
    ssmC = din("ssmC", [2, 2, 128, 16, 16])
    outT = nc.dram_tensor("outT", [D, TL], F32, kind="ExternalOutput").ap()

    hT = dscr("hT", [D, TL], F32)
    zF = dscr("zFl", [ZR, TL], BF16)
    zFg = dscr("zFg", [ZGR, TL], BF16)
    vtok = dscr("vtokl", [TL, 640], BF16)
    vtokg = dscr("vtokg", [2 * TL, 640], BF16)
    iwt = dscr("iwt", [TL, 8], F32)
    yprel = dscr("yprel", [512, T], F32)
    if inject:
        ycat = din("ycat", [2048, TL], BF16)
    else:
        ycat = dscr("ycat", [2048, TL], BF16)
    ypre = dscr("ypreg", [1024, T], F32)
    rope = dscr("rope", [2, 2, 64, TL], F32)
    Wi = dscr("Wi", [D, INW], BF16)
    Wo = dscr("Wo", [D, D], BF16)
    Wu = dscr("Wu", [D, DFF], BF16)
    Wd = dscr("Wd", [DFF, D], BF16)
    Wg = dscr("Wg", [D, D], BF16)
    Wp = dscr("Wp", [256, D], BF16)
    Wl = dscr("Wl", [1024, 1024], BF16)

    with ExitStack() as st:
        S = Sched(nc)
        b = B(nc, S)

        _nm = [0]

        def sb(stack, name, shape, dt):
            _nm[0] += 1
            return stack.enter_context(nc.sbuf_tensor("%s_%d" % (name, _nm[0]), list(shape), dt))

        ps = [st.enter_context(nc.psum_tensor("ps%d" % i, [128, 512], F32)) for i in range(7)]
        psx = st.enter_context(nc.psum_tensor("psx", [128, 512], F32))
        tpsx = Tk()
        tps = toks(7)
        ones = sb(st, "ones", [128, 128], BF16)
        ident = sb(st, "ident", [128, 128], BF16)
        identf = sb(st, "identf", [128, 128], F32)
        epsc = sb(st, "epsc", [128, 1], F32)
        vec = sb(st, "vec", [128, 2, NVEC], F32)
        tconst = Tk()
        b.memset("dve", ones[:], 1.0, [tconst])
        b.memset("dve", epsc[:], EPS, [tconst])
        b.memset("dve", identf[:], 1.0, [tconst])
        b.asel(identf[:], identf[:], [[-1, 128]], ALU.is_equal, 0.0, 0, 1, [], [tconst])
        b.cp("dve", ident[:], identf[:], [tconst], [tconst])
        b.dma(vec[:], vecs.rearrange("l p n -> p l n"), w=[tconst])
        qcol = sb(st, "qcol", [128, 16], F32)
        krow = sb(st, "krow", [128, 16], F32)
        selv = sb(st, "selv", [128, 2], F32)
        pio = sb(st, "pio", [128, 1], F32)
        b.dma(qcol[:], qcol_d[:, :], w=[tconst])
        b.dma(selv[:], selv_d[:, :], w=[tconst])
        b.ts("dve", krow[:], qcol[:], 1.0, ALU.add, [tconst], [tconst], s2=256.0, op1=ALU.min)
        b.iota(pio[:], [[0, 1]], [], [tconst], cm=1)

        def vcol(l, c0, n=1):
            return vec[:, l, c0:c0 + n]

        _sc_cache = {}

        def sincos(stack_, u, P, N, out_sin, out_cos, r, w, name, eng="dve"):
            if not hasattr(stack_, "sc_cache"):
                stack_.sc_cache = {}
            if N not in stack_.sc_cache:
                stack_.sc_cache[N] = (sb(stack_, name + "_ki", [128, N], I32), sb(stack_, name + "_kf", [128, N], F32),
                                      sb(stack_, name + "_rr", [128, N], F32), sb(stack_, name + "_mm", [128, N], F32), Tk())
            ki, kf, rr, mm_, tk = stack_.sc_cache[N]
            for shift, out in ((0.0, out_sin), (0.25, out_cos)):
                if shift == 0.0:
                    src = u
                    b.cp("act", ki[:P, :], u, r + [tk], [tk])
                else:
                    b.ts(eng, mm_[:P, :], u, shift, ALU.add, r + [tk], [tk])
                    src = mm_[:P, :]
                    b.cp("act", ki[:P, :], src, [tk], [tk])
                b.cp("act", kf[:P, :], ki[:P, :], [tk], [tk])
                b.tt(eng, rr[:P, :], src, kf[:P, :], ALU.subtract, r + [tk], [tk])
                b.act(out, rr[:P, :], AF.Sin, [tk], w + [tk], scale=TWO_PI_S)

        def rstd_from(stack_, name):
            sq = sb(stack_, name + "_sq", [128, 2, 4, TT], BF16)
            tsq = toks(2)
            sd = sb(stack_, name + "_sd", [128, TT], F32)
            rB = sb(stack_, name + "_rB", [128, TT], F32)
            tsd, trB = Tk(), Tk()

            def run(src, tsrc, hb, thb, psi):
                for g in range(KC // 4):
                    bi = g % 2
                    b.act(sq[:, bi, :, :], src[:, 4 * g:4 * g + 4, :], AF.Square, tsrc[4 * g:4 * g + 4], [tsq[bi]])
                    for k in range(4):
                        kc = 4 * g + k
                        b.mm(ps[psi][:], ones[:], sq[:, bi, k, :], kc == 0, kc == KC - 1, [tsq[bi], tconst], [tps[psi]])
                for kc in range(KC):
                    b.cp(("dve", "pool")[kc % 2], hb[:, kc, :], src[:, kc, :], [tsrc[kc]], [thb[kc]])
                b.act(sd[:], ps[psi][:], AF.Sqrt, [tps[psi], tconst], [tsd], scale=1.0 / D, bias=epsc[:, 0:1])
                b.recip(rB[:], sd[:], [tsd], [trB])
                return rB, trB
            return run

        with ExitStack() as s0:
            pi_ = sb(s0, "r_pi", [64, TL], I32)
            pf = sb(s0, "r_pf", [64, TL], F32)
            fr = sb(s0, "r_fr", [64, 2], F32)
            u = sb(s0, "r_u", [64, TL], F32)
            sn = sb(s0, "r_sn", [64, TL], F32)
            cs = sb(s0, "r_cs", [64, TL], F32)
            t1, t2 = Tk(), Tk()
            b.dma(pi_[:], posf[:, :], w=[t1])
            b.dma(fr[:], freqs[:, :], w=[t1])
            b.cp("dve", pf[:], pi_[:], [t1], [t1])
            for tb in range(2):
                P = 64 if tb == 0 else 32
                b.ts("dve", u[:P, :], pf[:P, :], fr[:P, tb:tb + 1], ALU.mult, [t1, t2], [t2],
                     s2=1.0 / (2 * math.pi), op1=ALU.mult)
                sincos(s0, u[:P, :], P, TL, sn[:P, :], cs[:P, :], [t2], [t2], "rsc")
                b.dma(rope[tb, 0, 0:P, :], cs[:P, :], r=[t2])
                b.dma(rope[tb, 1, 0:P, :], sn[:P, :], r=[t2])
        S.barrier()

        for L in range(n_layers):
            lam_init = 0.8 - 0.6 * math.exp(-0.3 * L)
            WIN_SEGS = [(AQ, 512, fAQ, 8), (AK, 512, fAK, 8), (BQ, 512, fBQ, 4), (BK, 128, fBK, 1),
                        (IQ, 512, fIQ, 8), (IK, 64, fIK, 1), (CU, 512, fCU, 0), (CU + 512, 512, fCU + 512, 0),
                        (AV, 512, vAV, 0), (BV, 128, vBV, 0), (IW, 8, vIW, 0)]

            def prep_list(plist, src, dst, K, segs, gcol, Lg, KG):
                nk = K // 128
                for (c0, n, d0, H) in segs:
                    for kg in range(0, nk, KG):
                        plist.append((src, dst, c0, n, d0, H, kg, min(KG, nk - kg), gcol, Lg))

            def p_load(piece, stg_i, tstg_i, q, qs=None):
                src, dst, c0, n, d0, H, kg, ng, gcol, Lg = piece
                b.dma(stg_i[:, 0:ng, 0:n], src[kg * 128:(kg + ng) * 128, c0:c0 + n].rearrange("(k p) n -> p k n", p=128),
                      w=[tstg_i], q=q)

            def p_run(piece, stg_i, tstg_i, wbf_i, twbf_i, q, engs, qs=None):
                src, dst, c0, n, d0, H, kg, ng, gcol, Lg = piece
                for k in range(ng):
                    eng = engs() if callable(engs) else engs
                    if H:
                        o_ = wbf_i[:, k, 0:n].rearrange("p (two h j) -> p two h j", two=2, h=H)
                        i_ = stg_i[:, k, 0:n].rearrange("p (h two j) -> p two h j", two=2, h=H)
                    else:
                        o_ = wbf_i[:, k, 0:n]
                        i_ = stg_i[:, k, 0:n]
                    if gcol is None:
                        if eng == "act" and H:
                            eng = "dve"
                        b.cp(eng, o_, i_, [tstg_i], [twbf_i])
                    else:
                        g = vcol(Lg, gcol + kg + k)
                        if eng == "act":
                            if H:
                                b.ts("dve", o_, i_, g, ALU.mult, [tstg_i, tconst], [twbf_i])
                            else:
                                b.act(o_, i_, AF.Copy, [tstg_i, tconst], [twbf_i], scale=g)
                        else:
                            b.ts(eng, o_, i_, g, ALU.mult, [tstg_i, tconst], [twbf_i])
                b.dma(dst[kg * 128:(kg + ng) * 128, d0:d0 + n].rearrange("(k p) n -> p k n", p=128),
                      wbf_i[:, 0:ng, 0:n], r=[twbf_i], q=(qs or q))

            if L == 0 and (stages is None or "P" in stages):
                with ExitStack() as s0:
                    NPB = 4
                    stg = [sb(s0, "p_stg%d" % i, [128, 8, 512], F32) for i in range(NPB)]
                    wbf = [sb(s0, "p_wbf%d" % i, [128, 8, 512], BF16) for i in range(NPB)]
                    tstg, twbf = toks(NPB), toks(NPB)
                    pl0 = []
                    prep_list(pl0, w_in[0], Wi, D, WIN_SEGS, 0, 0, 8)
                    p_load(pl0[0], stg[0], tstg[0], "sp")
                    p_load(pl0[1], stg[1], tstg[1], "sp")
                    for idx in range(len(pl0)):
                        if idx + 2 < len(pl0):
                            p_load(pl0[idx + 2], stg[(idx + 2) % NPB], tstg[(idx + 2) % NPB], "sp")
                        p_run(pl0[idx], stg[idx % NPB], tstg[idx % NPB], wbf[idx % NPB], twbf[idx % NPB], "sp", b.cast_eng)
                S.barrier()

            bgl = []
            if stages is None or "P" in stages:
                prep_list(bgl, w_glu[L], Wl, 1024, [(c, 512, c, 0) for c in range(0, 1024, 512)], None, L, 4)
                prep_list(bgl, w_out[L], Wo, D, [(c, 512, c, 0) for c in range(0, D, 512)], None, L, 4)
                prep_list(bgl, w_up[L], Wu, D, [(c, 512, c, 0) for c in range(0, DFF, 512)], 16, L, 4)
                prep_list(bgl, w_down[L], Wd, DFF, [(c, 512, c, 0) for c in range(0, D, 512)], None, L, 4)
                prep_list(bgl, w_gate[L], Wg, D, [(c, 512, c, 0) for c in range(0, D, 512)], 32, L, 4)
                prep_list(bgl, w_ple[L], Wp, 256, [(c, 512, c, 0) for c in range(0, D, 512)], None, L, 4)
                if L + 1 < n_layers:
                    prep_list(bgl, w_in[L + 1], Wi, D, WIN_SEGS, 0, L + 1, 4)
            bgs = {"loaded": 0, "run": 0, "bufs": None, "eng": "act", "qs": "act"}

            def bg_attach(stack_):
                nb = 3
                bgs["bufs"] = ([sb(stack_, "bg_stg%d" % i, [128, 4, 512], F32) for i in range(nb)], toks(nb),
                               [sb(stack_, "bg_wbf%d" % i, [128, 4, 512], BF16) for i in range(nb)], toks(nb))

            def bg_pump(k=1):
                stg_, tstg_, wbf_, twbf_ = bgs["bufs"]
                nb = len(stg_)
                for _ in range(k):
                    if bgs["run"] >= len(bgl):
                        return
                    while bgs["loaded"] < min(len(bgl), bgs["run"] + 2):
                        i = bgs["loaded"] % nb
                        p_load(bgl[bgs["loaded"]], stg_[i], tstg_[i], "sp")
                        bgs["loaded"] += 1
                    i = bgs["run"] % nb
                    p_run(bgl[bgs["run"]], stg_[i], tstg_[i], wbf_[i], twbf_[i], "sp", bgs["eng"], qs=bgs["qs"])
                    bgs["run"] += 1

            def bg_detach(all_=False):
                stg_, tstg_, wbf_, twbf_ = bgs["bufs"]
                nb = len(stg_)
                while bgs["run"] < (len(bgl) if all_ else bgs["loaded"]):
                    if all_:
                        bg_pump(1)
                    else:
                        i = bgs["run"] % nb
                        p_run(bgl[bgs["run"]], stg_[i], tstg_[i], wbf_[i], twbf_[i], "sp", bgs["eng"], qs=bgs["qs"])
                        bgs["run"] += 1
                bgs["bufs"] = None

            if stages is None or "1" in stages:
                src_h = xT if L == 0 else hT
                with ExitStack() as s0:
                    ht = sb(s0, "a_ht", [128, KC, TT], F32)
                    hb = sb(s0, "a_hb", [128, KC, TT], BF16)
                    tht, thb = toks(KC), toks(KC)
                    wt = [sb(s0, "a_wt%d" % i, [128, KC, 512], BF16) for i in range(2)]
                    twt = toks(2)
                    rt = sb(s0, "a_rt", [128, 2, 2, TT], F32)
                    rtR = sb(s0, "a_rtR", [128, 2, 2, TT], F32)
                    trt, trtR = Tk(), Tk()
                    tmp = [sb(s0, "a_tmp%d" % i, [128, TT], F32) for i in range(4)]
                    ttmp = toks(4)
                    ob = [sb(s0, "a_ob%d" % i, [128, TT], BF16) for i in range(4)]
                    tob = toks(4)
                    rcol = sb(s0, "a_rcol", [128, 4], F32)
                    trcol = Tk()
                    vo = [sb(s0, "a_vo%d" % i, [128, 640], BF16) for i in range(2)]
                    tvo = toks(2)
                    iwo = [sb(s0, "a_iwo%d" % i, [128, 8], F32) for i in range(2)]
                    tiwo = toks(2)
                    rs = rstd_from(s0, "a_rs")
                    wcnt = [0]
                    obc = [0]

                    FM_LOADS = [(0, 512), (512, 512), (1024, 512), (1536, 128), (1664, 512), (2176, 64), (2240, 512), (2752, 512)]
                    jobs = []

                    def rope_jobs(fbase, zbase, H, dh, table):
                        hs = H * dh // 2
                        M = min(128, hs)
                        for p_ in range(hs // M):
                            nh = M // (dh // 2)
                            jobs.append(("rope", fbase + p_ * M, fbase + hs + p_ * M, M, table, zbase, dh, nh, p_ * nh))
                    rope_jobs(fAQ, 0, 8, 64, 1)
                    rope_jobs(fAK, 512, 8, 64, 1)
                    rope_jobs(fBQ, 1024, 4, 128, 0)
                    rope_jobs(fBK, 1536, 1, 128, 0)
                    rope_jobs(fIQ, 1664, 8, 64, 1)
                    rope_jobs(fIK, 2176, 1, 64, 1)
                    for c in range(8):
                        jobs.append(("plain", fCU + c * 128, 128, 2240 + c * 128))

                    for tt_ in range(NTL):
                        t0 = tt_ * TT
                        for kc in range(KC):
                            b.dma(ht[:, kc, :], src_h[kc * 128:(kc + 1) * 128, t0:t0 + TT], w=[tht[kc]])
                        for tb in range(2):
                            nj = 64 if tb == 0 else 32
                            for rep in range(128 // nj):
                                for c_ in range(2):
                                    b.dma(rt[rep * nj:(rep + 1) * nj, tb, c_, :], rope[tb, c_, 0:nj, t0:t0 + TT], w=[trt])
                        rB, trB = rs(ht, tht, hb, thb, 6)
                        for tb in range(2):
                            for c_ in range(2):
                                b.tt("pool", rtR[:, tb, c_, :], rt[:, tb, c_, :], rB[:], ALU.mult, [trt, trB], [trtR])
                        for sbk in range(4):
                            b.tr(ps[5][:, sbk * 128:(sbk + 1) * 128], rB[:, sbk * 128:(sbk + 1) * 128], identf[:],
                                 [trB, tconst], [tps[5]])
                        b.cp("dve", rcol[:, :], ps[5][:, 0:512:128], [tps[5]], [trcol])

                        loaded = {}

                        def load_w(li):
                            c0, n = FM_LOADS[li]
                            i = wcnt[0] % 2
                            wcnt[0] += 1
                            b.dma(wt[i][:, :, 0:n], Wi[:, c0:c0 + n].rearrange("(k p) n -> p k n", p=128), w=[twt[i]])
                            loaded[li] = i

                        def lidx(col):
                            for li_, (c0_, n_) in enumerate(FM_LOADS):
                                if c0_ <= col < c0_ + n_:
                                    return li_
                            raise ValueError(col)

                        def wcols(col, M):
                            li = lidx(col)
                            return wt[loaded[li]], col - FM_LOADS[li][0]

                        def job_loads(j):
                            if j[0] == "rope":
                                return {lidx(j[1]), lidx(j[2])}
                            return {lidx(j[1])}
                        load_w(0)
                        next_load = 1
                        psi = 0
                        for j in jobs:
                            need = max(job_loads(j))
                            while next_load <= need:
                                load_w(next_load)
                                next_load += 1
                            if j[0] == "rope":
                                _, c1, c2, M, tb, zb, dh, nh, h0 = j
                                pA, pB = psi % 4, (psi + 1) % 4
                                psi += 2
                                for (col, pi2) in ((c1, pA), (c2, pB)):
                                    wtile, off = wcols(col, M)
                                    wi_ = loaded[lidx(col)]
                                    for kc in range(KC):
                                        b.mm(ps[pi2][:M, :], wtile[:, kc, off:off + M], hb[:, kc, :], kc == 0, kc == KC - 1,
                                             [twt[wi_], thb[kc]], [tps[pi2]])
                                cosR, sinR = rtR[:M, tb, 0, :], rtR[:M, tb, 1, :]
                                b.tt("dve", tmp[0][:M, :], ps[pA][:M, :], cosR, ALU.mult, [tps[pA], trtR], [ttmp[0]])
                                b.tt("dve", tmp[1][:M, :], ps[pB][:M, :], sinR, ALU.mult, [tps[pB], trtR], [ttmp[1]])
                                b.tt("dve", tmp[2][:M, :], ps[pB][:M, :], cosR, ALU.mult, [tps[pB], trtR], [ttmp[2]])
                                b.tt("dve", tmp[3][:M, :], ps[pA][:M, :], sinR, ALU.mult, [tps[pA], trtR], [ttmp[3]])
                                o1, o2 = obc[0] % 4, (obc[0] + 1) % 4
                                obc[0] += 2
                                b.tt("dve", ob[o1][:M, :], tmp[0][:M, :], tmp[1][:M, :], ALU.subtract, [ttmp[0], ttmp[1]], [tob[o1]])
                                b.tt("dve", ob[o2][:M, :], tmp[2][:M, :], tmp[3][:M, :], ALU.add, [ttmp[2], ttmp[3]], [tob[o2]])
                                hd = dh // 2
                                for hl in range(nh):
                                    row = zb + (h0 + hl) * dh
                                    b.dma(zF[row:row + hd, t0:t0 + TT], ob[o1][hl * hd:(hl + 1) * hd, :], r=[tob[o1]])
                                    b.dma(zF[row + hd:row + dh, t0:t0 + TT], ob[o2][hl * hd:(hl + 1) * hd, :], r=[tob[o2]])
                            else:
                                _, col, M, zrow = j
                                pA = psi % 4
                                psi += 1
                                wtile, off = wcols(col, M)
                                wi_ = loaded[lidx(col)]
                                for kc in range(KC):
                                    b.mm(ps[pA][:M, :], wtile[:, kc, off:off + M], hb[:, kc, :], kc == 0, kc == KC - 1,
                                         [twt[wi_], thb[kc]], [tps[pA]])
                                o1 = obc[0] % 4
                                obc[0] += 1
                                b.tt("dve", ob[o1][:M, :], ps[pA][:M, :], rB[:M, :], ALU.mult, [tps[pA], trB], [tob[o1]])
                                b.dma(zF[zrow:zrow + M, t0:t0 + TT], ob[o1][:M, :], r=[tob[o1]])

                        i1 = wcnt[0] % 2
                        wcnt[0] += 1
                        b.dma(wt[i1][:, :, 0:512], Wi[:, vAV:vAV + 512].rearrange("(k p) n -> p k n", p=128), w=[twt[i1]])
                        i2 = wcnt[0] % 2
                        wcnt[0] += 1
                        b.dma(wt[i2][:, :, 0:136], Wi[:, vBV:vBV + 136].rearrange("(k p) n -> p k n", p=128), w=[twt[i2]])
                        for sbk in range(4):
                            vi = sbk % 2
                            pA, pB = psi % 4, (psi + 1) % 4
                            psi += 2
                            for kc in range(KC):
                                b.mm(ps[pA][:, :], hb[:, kc, sbk * 128:(sbk + 1) * 128], wt[i1][:, kc, 0:512], kc == 0, kc == KC - 1,
                                     [twt[i1], thb[kc]], [tps[pA]])
                            for kc in range(KC):
                                b.mm(ps[pB][:, 0:136], hb[:, kc, sbk * 128:(sbk + 1) * 128], wt[i2][:, kc, 0:136], kc == 0, kc == KC - 1,
                                     [twt[i2], thb[kc]], [tps[pB]])
                            b.ts("dve", vo[vi][:, 0:512], ps[pA][:, :], rcol[:, sbk:sbk + 1], ALU.mult, [tps[pA], trcol], [tvo[vi]])
                            b.ts("dve", vo[vi][:, 512:640], ps[pB][:, 0:128], rcol[:, sbk:sbk + 1], ALU.mult, [tps[pB], trcol], [tvo[vi]])
                            b.ts("dve", iwo[vi][:, :], ps[pB][:, 128:136], rcol[:, sbk:sbk + 1], ALU.mult, [tps[pB], trcol], [tiwo[vi]],
                                 s2=(8 ** -0.5) * (64 ** -0.5), op1=ALU.mult)
                            b.dma(vtok[t0 + sbk * 128:t0 + (sbk + 1) * 128, :], vo[vi][:, :], r=[tvo[vi]])
                            b.dma(iwt[t0 + sbk * 128:t0 + (sbk + 1) * 128, :], iwo[vi][:, :], r=[tiwo[vi]])
                S.barrier()
                tzg = toks(NZC)
                tvg = toks(TL // VCH)
                for k_ in range(TL // VCH):
                    b.allgather(vtokg[k_ * 2 * VCH:(k_ + 1) * 2 * VCH, :], vtok[k_ * VCH:(k_ + 1) * VCH, :], w=[tvg[k_]])
                for k_ in range(NZC):
                    rows_k = min(RCH, ZR - k_ * RCH)
                    b.allgather(zFg[k_ * 2 * RCH:k_ * 2 * RCH + 2 * rows_k, :], zF[k_ * RCH:k_ * RCH + rows_k, :], w=[tzg[k_]])


            if stages is None or "2" in stages:
                with ExitStack() as s0:
                    asets = [([sb(s0, "A_k%d_%d" % (j_, i), [65, T], BF16) for i in range(2)],
                              [sb(s0, "A_q%d_%d" % (j_, i), [65, TL], BF16) for i in range(2)],
                              sb(s0, "A_v%d" % j_, [128, 32, 128], BF16), toks(2), toks(2), Tk()) for j_ in range(2)]
                    sqb = sb(s0, "A_sqb", [64, T], BF16)
                    tsqb = Tk()
                    sel = sb(s0, "A_sel", [64, 65], BF16)
                    kmx = sb(s0, "A_kmx", [65, 8], F32)
                    kmax = sb(s0, "A_kmax", [65, 1], F32)
                    tkm = Tk()
                    lmv = sb(s0, "A_lmv", [128, 4, 64], F32)
                    ltmp = sb(s0, "A_ltmp", [128, 64], F32)
                    lsc = sb(s0, "A_lsc", [128, 4], F32)
                    tl = Tk()
                    pT_ = [sb(s0, "A_pT%d" % i, [128, TT], BF16) for i in range(3)]
                    tpT = toks(3)
                    rl = sb(s0, "A_rl", [128, TT], F32)
                    trl = Tk()
                    oc = [sb(s0, "A_oc%d" % i, [128, TT], F32) for i in range(2)]
                    toc = toks(2)
                    dif = sb(s0, "A_dif", [128, TT], F32)
                    dsq = sb(s0, "A_dsq", [128, TT], BF16)
                    dsd = sb(s0, "A_dsd", [128, TT], F32)
                    yo = [sb(s0, "A_yo%d" % i, [128, TT], BF16) for i in range(2)]
                    tdif, tdsq, tdsd = Tk(), Tk(), Tk()
                    tyo = toks(2)
                    bg_attach(s0)
                    bgs["eng"], bgs["qs"] = "dve", "sp"
                    qidxB = sb(s0, "A_qidxB", [128, TL], F32)
                    bm = sb(s0, "A_bm", [128, NTL, 8, TT], BF16)
                    tbm = Tk()
                    b.dma(qidxB[:], qidxB_d[:, :], w=[tbm])
                    for i_ in range(NTL):
                        gmin_ = min(G[0][i_], G[1][i_])
                        for j_ in range(8):
                            b.ts("dve", bm[:, i_, j_, :], qidxB[:, i_ * TT:(i_ + 1) * TT], pio[:, 0:1], ALU.subtract, [tbm, tconst], [tbm],
                                 s2=float((4 * gmin_ + j_) * 128), op1=ALU.is_ge)
                    b.dma(lmv[:], lamv[L], w=[tl])
                    for i in range(2):
                        b.tt("dve", ltmp[:], lmv[:, 2 * i, :], lmv[:, 2 * i + 1, :], ALU.mult, [tl], [tl])
                        b.red(lsc[:, i:i + 1], ltmp[:], ALU.add, [tl], [tl])
                    b.act(lsc[:, 0:2], lsc[:, 0:2], AF.Exp, [tl], [tl])
                    b.tt("dve", lsc[:, 2:3], lsc[:, 1:2], lsc[:, 0:1], ALU.subtract, [tl], [tl])
                    b.ts("dve", lsc[:, 2:3], lsc[:, 2:3], -lam_init, ALU.add, [tl], [tl])
                    b.ts("dve", lsc[:, 3:4], vcol(L, 72), 1.0 - lam_init, ALU.mult, [tl, tconst], [tl])
                    b.memset("dve", sel[:], 0.0, [tl])
                    b.memset("dve", sel[:, 64:65], 1.0, [tl])
                    for j_ in range(2):
                        for c in range(2):
                            b.memset("dve", asets[j_][0][c][64:65, :], -1.0, [asets[j_][3][c]])
                    scale = 64 ** -0.5
                    pcnt = [0]

                    def prologue(h):
                        kA, qA, vA, tkA, tqA, tvA = asets[h % 2]
                        for g_ in range(8):
                            r0 = vg_row(RG[g_], LG[g_] * TT)
                            b.dma(vA[:, 4 * g_:4 * g_ + 4, :],
                                  vtokg[r0:r0 + TT, h * 128:(h + 1) * 128].rearrange("(c p) e -> p c e", p=128),
                                  r=[tvg[(LG[g_] * TT) // VCH]], w=[tvA])
                        for c in range(2):
                            hc = 2 * h + c
                            for g_ in range(8):
                                (zr, zn, zo), = zg_pieces(RG[g_], 512 + hc * 64, 64)
                                b.dma(kA[c][0:64, g_ * TT:(g_ + 1) * TT], zFg[zr:zr + 64, LG[g_] * TT:(LG[g_] + 1) * TT],
                                      r=[tzg[(512 + hc * 64) // RCH]], w=[tkA[c]])
                            b.dma(qA[c][0:64, :], zF[hc * 64:(hc + 1) * 64, :], w=[tqA[c]])
                            b.act(sqb[:], kA[c][0:64, :], AF.Square, [tkA[c]], [tsqb])
                            for t8 in range(8):
                                b.mm(psx[0:65, :], sel[:], sqb[:, t8 * TT:(t8 + 1) * TT], True, True, [tsqb, tl], [tpsx])
                                b.red(kmx[64:65, t8:t8 + 1], psx[64:65, :], ALU.max, [tpsx], [tkm])
                            b.red(kmax[64:65, :], kmx[64:65, :], ALU.max, [tkm], [tkm])
                            b.act(sqb[:, 0:TL], qA[c][0:64, :], AF.Square, [tqA[c]], [tsqb])
                            for t8 in range(NTL):
                                b.mm(psx[0:65, :], sel[:], sqb[:, t8 * TT:(t8 + 1) * TT], True, True, [tsqb, tl], [tpsx])
                                b.act(qA[c][64:65, t8 * TT:(t8 + 1) * TT], psx[64:65, :], AF.Sqrt, [tpsx, tkm], [tqA[c]],
                                      scale=kmax[64:65, 0:1])
                    prologue(0)
                    for h in range(4):
                        kA, qA, vA, tkA, tqA, tvA = asets[h % 2]
                        gmx = [max(G[0][i_], G[1][i_]) for i_ in range(NTL)]
                        gmn = [min(G[0][i_], G[1][i_]) for i_ in range(NTL)]
                        steps = [(qt, c, sc) for qt in range(NTL) for c in range(2) for sc in range(4 * gmx[qt] + 4)]
                        base = pcnt[0]

                        def issueS(k):
                            qt, c, sc = steps[k]
                            pa = (base + k) % 3
                            b.mm(ps[pa][:], kA[c][:, sc * 128:(sc + 1) * 128], qA[c][:, qt * TT:(qt + 1) * TT], True, True,
                                 [tkA[c], tqA[c]], [tps[pa]])
                        issueS(0)
                        issueS(1)
                        for k, (qt, c, sc) in enumerate(steps):
                            q0 = qt * TT
                            po, pl = ps[3 + c], ps[5 + c]
                            nsc = 4 * gmx[qt] + 4
                            pa = (base + k) % 3
                            b.act(pT_[pa][:], ps[pa][:], AF.Exp, [tps[pa]], [tpT[pa]], scale=scale)
                            if sc >= 4 * gmn[qt]:
                                b.tt("dve", pT_[pa][:], pT_[pa][:], bm[:, qt, sc - 4 * gmn[qt], :], ALU.mult, [tpT[pa], tbm], [tpT[pa]])
                            b.mm(po[:], vA[:, sc, :], pT_[pa][:], sc == 0, sc == nsc - 1, [tvA, tpT[pa]], [tps[3 + c]])
                            b.mm(pl[:], ones[:], pT_[pa][:], sc == 0, sc == nsc - 1, [tconst, tpT[pa]], [tps[5 + c]])
                            if k + 2 < len(steps):
                                issueS(k + 2)
                            if k % 6 == 5:
                                bg_pump(1)
                            if k == len(steps) // 3 and h + 1 < 4:
                                prologue(h + 1)
                            if sc == nsc - 1:
                                b.recip(rl[:], pl[:], [tps[5 + c]], [trl])
                                b.tt("dve", oc[c][:], po[:], rl[:], ALU.mult, [tps[3 + c], trl], [toc[c]])
                                if c == 1:
                                    b.stt(dif[:], oc[1][:], lsc[:, 2:3], oc[0][:], ALU.mult, ALU.add, [toc[0], toc[1], tl], [tdif])
                                    b.act(dsq[:], dif[:], AF.Square, [tdif], [tdsq])
                                    b.mm(psx[:], ones[:], dsq[:], True, True, [tconst, tdsq], [tpsx])
                                    b.act(dsd[:], psx[:], AF.Sqrt, [tpsx, tconst], [tdsd], scale=1.0 / 128, bias=epsc[:, 0:1])
                                    b.recip(dsd[:], dsd[:], [tdsd], [tdsd])
                                    yi = qt % 2
                                    b.stt(yo[yi][:], dif[:], lsc[:, 3:4], dsd[:], ALU.mult, ALU.mult, [tdif, tdsd, tl], [tyo[yi]])
                                    b.dma(ycat[h * 128:(h + 1) * 128, q0:q0 + TT], yo[yi][:], r=[tyo[yi]])
                        pcnt[0] = base + len(steps)
                    bg_detach()
                S.barrier()


            if stages is None or "3" in stages:
                with ExitStack() as s0:
                    ik = sb(s0, "B_ik", [64, T], BF16)
                    iw_ = sb(s0, "B_iw", [128, 16, 8], F32)
                    dW = [sb(s0, "B_dW%d" % i, [128, 8, 128], F32) for i in range(2)]
                    tdW = toks(2)
                    pen = sb(s0, "B_pen", [128, 1024], F32)
                    tpen = Tk()
                    bk = sb(s0, "B_bk", [128, T], BF16)
                    bv = sb(s0, "B_bv", [128, 32, 128], BF16)
                    tik, tiw, tbk, tbv = Tk(), Tk(), Tk(), Tk()
                    iq = [sb(s0, "B_iq%d" % i, [64, 8, TT], BF16) for i in range(2)]
                    bq = [sb(s0, "B_bq%d" % i, [128, 4, TT], BF16) for i in range(2)]
                    tiq, tbq = toks(2), toks(2)
                    Sc = [sb(s0, "B_Sc%d" % i, [128, T], F32) for i in range(2)]
                    tSc = toks(2)
                    junk = sb(s0, "B_junk", [128, T], BF16)
                    tjunk = Tk()
                    msk = [sb(s0, "B_msk%d" % i, [128, T], F32) for i in range(2)]
                    tmsk = toks(2)
                    mT = sb(s0, "B_mT", [128, 32, TT], BF16)
                    tmT = Tk()
                    tmp = [sb(s0, "B_tmp%d" % i, [128, TT], F32) for i in range(3)]
                    ttmp = toks(3)
                    sm = [sb(s0, "B_sm%d" % i, [128, 8], F32) for i in range(2)]
                    steps = [sb(s0, "B_steps%d" % i, [128, 16], F32) for i in range(2)]
                    pow2 = sb(s0, "B_pow2", [128, 16], F32)
                    stp2 = [sb(s0, "B_stp2_%d" % i, [128, 16], F32) for i in range(2)]
                    tsm = toks(2)
                    tp2 = Tk()
                    pT_ = [sb(s0, "B_pT%d" % i, [128, TT], BF16) for i in range(3)]
                    tpT = toks(3)
                    rl = sb(s0, "B_rl", [128, TT], F32)
                    trl = Tk()
                    yo = [sb(s0, "B_yo%d" % i, [128, TT], BF16) for i in range(2)]
                    tyo = toks(2)
                    sqk = sb(s0, "B_sqk", [128, TT], BF16)
                    tsqk = Tk()
                    kmx = sb(s0, "B_kmx", [1, 9], F32)
                    tkm = Tk()
                    mrow = [sb(s0, "B_mrow%d" % i, [1, 4, TT], BF16) for i in range(2)]
                    tmrow = toks(2)
                    negone = sb(s0, "B_negone", [1, 128], BF16)
                    tneg = Tk()
                    NIT = 16
                    kio = sb(s0, "B_kio", [128, T], F32)
                    tkio = Tk()
                    b.iota(kio[:], [[1, T]], [], [tkio])
                    for g_ in range(8):
                        cs_ = slice(LG[g_] * TT, (LG[g_] + 1) * TT)
                        (zr, zn, zo), = zg_pieces(RG[g_], 2176, 64)
                        b.dma(ik[:, g_ * TT:(g_ + 1) * TT], zFg[zr:zr + 64, cs_], r=[tzg[2176 // RCH]], w=[tik])
                        (zr, zn, zo), = zg_pieces(RG[g_], 1536, 128)
                        b.dma(bk[:, g_ * TT:(g_ + 1) * TT], zFg[zr:zr + 128, cs_], r=[tzg[1536 // RCH]], w=[tbk])
                        r0 = vg_row(RG[g_], LG[g_] * TT)
                        b.dma(bv[:, 4 * g_:4 * g_ + 4, :], vtokg[r0:r0 + TT, 512:640].rearrange("(c p) e -> p c e", p=128),
                              r=[tvg[(LG[g_] * TT) // VCH]], w=[tbv])
                    b.dma(iw_[:], iwt.rearrange("(c p) h -> p c h", p=128), w=[tiw])
                    for k in range(NIT):
                        b.memset("pool", pow2[:, k:k + 1], 2.0 ** -(k + 1), [tp2])
                    b.memset("pool", negone[:], -1.0, [tneg])
                    for t8 in range(8):
                        b.act(sqk[:], bk[:, t8 * TT:(t8 + 1) * TT], AF.Square, [tbk], [tsqk])
                        b.mm(ps[t8 % 3][0:1, :], ones[:, 0:1], sqk[:], True, True, [tsqk, tconst], [tps[t8 % 3]])
                        b.red(kmx[0:1, t8:t8 + 1], ps[t8 % 3][0:1, :], ALU.max, [tps[t8 % 3]], [tkm])
                    b.red(kmx[0:1, 8:9], kmx[0:1, 0:8], ALU.max, [tkm], [tkm])
                    pcnt = [0]
                    tcnt = [0]
                    hbk = [0]

                    def tile_info(qt):
                        return max(G[0][qt], G[1][qt]), min(G[0][qt], G[1][qt])

                    def gen_scores(qb):
                        qt, ql = qb // 4, qb % 4
                        gmax, gmin = tile_info(qt)
                        bi = qt % 2
                        q0 = qt * TT
                        if ql == 0:
                            b.dma(iq[bi][:], zF[1664:2176, q0:q0 + TT].rearrange("(h d) t -> d h t", d=64), w=[tiq[bi]])
                            b.dma(bq[bi][:], zF[1024:1536, q0:q0 + TT].rearrange("(h d) t -> d h t", d=128), w=[tbq[bi]])
                            for h in range(4):
                                b.act(sqk[:], bq[bi][:, h, :], AF.Square, [tbq[bi]], [tsqk])
                                pa = pcnt[0] % 3
                                pcnt[0] += 1
                                b.mm(ps[pa][0:1, :], ones[:, 0:1], sqk[:], True, True, [tsqk, tconst], [tps[pa]])
                                b.act(mrow[bi][0:1, h, :], ps[pa][0:1, :], AF.Sqrt, [tps[pa], tkm], [tmrow[bi]], scale=kmx[0:1, 8:9])
                        n = (4 * gmax + ql + 1) * 128
                        si = qb % 2
                        sc_, tsc_ = Sc[si], tSc[si]
                        for h in range(8):
                            b.act(dW[si][:, h, :], identf[:], AF.Copy, [tconst, tiw], [tdW[si]], scale=iw_[:, qb, h:h + 1])
                        ssteps = [(st_, h) for st_ in range((n + 511) // 512) for h in range(8)]
                        sbase = pcnt[0]

                        def issueI(k):
                            st_, h = ssteps[k]
                            c0 = st_ * 512
                            ncol = min(512, n - c0)
                            pa = (sbase + k) % 3
                            b.mm(ps[pa][:, 0:ncol], iq[bi][:, h, ql * 128:(ql + 1) * 128], ik[:, c0:c0 + ncol], True, True,
                                 [tiq[bi], tik], [tps[pa]])
                        issueI(0)
                        issueI(1)
                        for k, (st_, h) in enumerate(ssteps):
                            c0 = st_ * 512
                            ncol = min(512, n - c0)
                            pa = (sbase + k) % 3
                            ti = (sbase + k) % 3
                            acc, tacc = (ps[3], tps[3]) if st_ % 2 == 0 else (ps[4], tps[4])
                            b.act(tmp[ti][:, 0:ncol], ps[pa][:, 0:ncol], AF.Relu, [tps[pa]], [ttmp[ti]])
                            b.mm(acc[:, 0:ncol], dW[si][:, h, :], tmp[ti][:, 0:ncol], h == 0, h == 7, [tdW[si], ttmp[ti]], [tacc])
                            if k + 2 < len(ssteps):
                                issueI(k + 2)
                            if h == 7:
                                b.cp("dve", sc_[:, c0:c0 + ncol], acc[:, 0:ncol], [tacc], [tsc_])
                            yield
                        pcnt[0] = sbase + len(ssteps)

                    def gen_select(qb):
                        qt, ql = qb // 4, qb % 4
                        gmax, gmin = tile_info(qt)
                        n = (4 * gmax + ql + 1) * 128
                        k0 = 4 * gmin * 128
                        si = qb % 2
                        sc_, tsc_ = Sc[si], tSc[si]
                        sm_, tsm_ = sm[si], tsm[si]
                        stp = steps[si]
                        b.red(sm_[:, 0:1], sc_[:, 0:n], ALU.min, [tsc_], [tsm_])
                        b.ts("dve", pen[:, 0:n - k0], kio[:, k0:n], qcol[:, qb:qb + 1], ALU.is_gt, [tconst, tkio, tsm_], [tpen],
                             s2=NEG, op1=ALU.mult)
                        b.tt("pool", sc_[:, k0:n], sc_[:, k0:n], pen[:, 0:n - k0], ALU.add, [tsc_, tpen], [tsc_])
                        yield
                        b.red(sm_[:, 1:2], sc_[:, 0:n], ALU.max, [tsc_], [tsm_])
                        b.tt("dve", sm_[:, 2:3], sm_[:, 1:2], sm_[:, 0:1], ALU.subtract, [tsm_], [tsm_])
                        b.ts("dve", stp[:, :], pow2[:, :], sm_[:, 2:3], ALU.mult, [tsm_, tp2], [tsm_], s2=0.5, op1=ALU.mult)
                        b.ts("dve", stp2[si][:, :], pow2[:, :], sm_[:, 2:3], ALU.mult, [tsm_, tp2], [tsm_])
                        b.stt(sm_[:, 4:5], sm_[:, 2:3], 0.5, sm_[:, 0:1], ALU.mult, ALU.add, [tsm_], [tsm_])
                        yield
                        for k in range(NIT):
                            b.ts("dve", junk[:, 0:n], sc_[:, 0:n], sm_[:, 4:5], ALU.is_ge, [tsc_, tsm_], [tjunk, tsm_],
                                 s2=0.0, op1=ALU.add, accum=sm_[:, 5:6])
                            yield
                            b.stt(sm_[:, 6:7], sm_[:, 5:6], krow[:, qb:qb + 1], stp2[si][:, k:k + 1], ALU.is_ge, ALU.mult, [tsm_, tconst], [tsm_])
                            yield
                            b.stt(sm_[:, 4:5], sm_[:, 6:7], stp[:, k:k + 1], sm_[:, 4:5], ALU.subtract, ALU.add, [tsm_], [tsm_])
                            yield
                        b.tt("dve", sm_[:, 3:4], sm_[:, 4:5], stp2[si][:, NIT - 1:NIT], ALU.subtract, [tsm_], [tsm_])
                        mk, tmk = msk[si], tmsk[si]
                        b.ts("dve", mk[:, 0:n], sc_[:, 0:n], sm_[:, 3:4], ALU.is_ge, [tsc_, tsm_], [tmk])
                        if ql < 3:
                            b.memset("pool", mT[:, n // 128:4 * gmax + 4, ql * 128:(ql + 1) * 128], 0.0, [tmT])
                        for j0 in range(0, n // 128, 4):
                            nj = min(4, n // 128 - j0)
                            tb_, ttb_ = (ps[5], tps[5]) if hbk[0] % 2 == 0 else (psx, tpsx)
                            hbk[0] += 1
                            for j in range(nj):
                                b.tr(tb_[:, j * 128:(j + 1) * 128], mk[:, (j0 + j) * 128:(j0 + j + 1) * 128], identf[:],
                                     [tmk, tconst], [ttb_])
                            for j in range(nj):
                                b.cp(("dve", "act")[j % 2], mT[:, j0 + j, ql * 128:(ql + 1) * 128], tb_[:, j * 128:(j + 1) * 128],
                                     [ttb_], [tmT])
                            yield

                    def attention(qt):
                        gmax, gmin = tile_info(qt)
                        bi = qt % 2
                        q0 = qt * TT
                        nsc = 4 * gmax + 4
                        psteps = [(h, sc) for h in range(4) for sc in range(nsc)]
                        base = pcnt[0]

                        def issueS(k):
                            h, sc = psteps[k]
                            pa = (base + k) % 3
                            b.mm(ps[pa][:], bk[:, sc * 128:(sc + 1) * 128], bq[bi][:, h, :], True, False, [tbk, tbq[bi]], [tps[pa]])
                            b.mm(ps[pa][:], negone[0:1, :], mrow[bi][0:1, h, :], False, True, [tneg, tmrow[bi]], [tps[pa]])
                        issueS(0)
                        issueS(1)
                        for k, (h, sc) in enumerate(psteps):
                            po, pl = (ps[5], ps[6]) if h % 2 == 0 else (psx, ps[6])
                            po, tpo = (ps[5], tps[5]) if h % 2 == 0 else (psx, tpsx)
                            pl, tpl = ps[6], tps[6]
                            pa = (base + k) % 3
                            b.act(pT_[pa][:], ps[pa][:], AF.Exp, [tps[pa]], [tpT[pa]], scale=128 ** -0.5)
                            b.tt("dve", pT_[pa][:], pT_[pa][:], mT[:, sc, :], ALU.mult, [tpT[pa], tmT], [tpT[pa]])
                            b.mm(po[:], bv[:, sc, :], pT_[pa][:], sc == 0, sc == nsc - 1, [tbv, tpT[pa]], [tpo])
                            b.mm(pl[:], ones[:], pT_[pa][:], sc == 0, sc == nsc - 1, [tconst, tpT[pa]], [tpl])
                            if k + 2 < len(psteps):
                                issueS(k + 2)
                            if sc == nsc - 1:
                                b.recip(rl[:], pl[:], [tpl], [trl])
                                yi = h % 2
                                b.tt("dve", yo[yi][:], po[:], rl[:], ALU.mult, [tpo, trl], [tyo[yi]])
                                b.dma(ycat[512 + h * 128:512 + (h + 1) * 128, q0:q0 + TT], yo[yi][:], r=[tyo[yi]])
                        pcnt[0] = base + len(psteps)

                    NQB = 4 * NTL
                    for _ in gen_scores(0):
                        pass
                    for qb in range(NQB):
                        gsel = gen_select(qb)
                        gsc = gen_scores(qb + 1) if qb + 1 < NQB else iter(())
                        nsteps_sc = 8 * ((( 4 * tile_info((qb + 1) // 4)[0] + (qb + 1) % 4 + 1) * 128 + 511) // 512) if qb + 1 < NQB else 0
                        per = max(1, -(-nsteps_sc // (4 * NIT)))
                        done_sc = False
                        for _ in gsel:
                            for _i in range(per):
                                if next(gsc, None) is None and True:
                                    pass
                        for _ in gsc:
                            pass
                        if qb % 4 == 3:
                            attention(qb // 4)
                S.barrier()


            if stages is None or "4" in stages:
                with ExitStack() as s0:
                    BreT = sb(s0, "C_BreT", [128, 16, 128], BF16)
                    BimT = sb(s0, "C_BimT", [128, 16, 128], BF16)
                    CreT = sb(s0, "C_CreT", [128, 16, 128], BF16)
                    nCreT = sb(s0, "C_nCreT", [128, 16, 128], BF16)
                    nCimT = sb(s0, "C_nCimT", [128, 16, 128], BF16)
                    tW = Tk()
                    mag2 = sb(s0, "C_mag2", [128, 16], F32)
                    thn2 = sb(s0, "C_thn2", [128, 16], F32)
                    cr2 = sb(s0, "C_cr2", [128, 16], F32)
                    sr2 = sb(s0, "C_sr2", [128, 16], F32)
                    nsr2 = sb(s0, "C_nsr2", [128, 16], F32)
                    tP2 = Tk()
                    iof = sb(s0, "C_iof", [128, TT], F32)
                    tio = Tk()
                    with ExitStack() as s1:
                        pr = [sb(s1, "C_pr%d" % i, [128, 256], F32) for i in range(14)]
                        lr, li, ls, Br, Bi, t1_, t2_, sn, cs, mg, fre, fim, bbr, bbi = pr
                        tq = Tk()
                        for i, tl_ in enumerate((lr, li, ls)):
                            b.dma(tl_[:], ssmL1[L, i].rearrange("p o q -> p (o q)"), w=[tq])
                        b.dma(Br[:], ssmB[L, 0].rearrange("p o q -> p (o q)"), w=[tq])
                        b.dma(Bi[:], ssmB[L, 1].rearrange("p o q -> p (o q)"), w=[tq])
                        b.act(ls[:], ls[:], AF.Exp, [tq], [tq])
                        b.tt("dve", t1_[:], li[:], ls[:], ALU.mult, [tq], [tq])
                        b.ts("dve", t1_[:], t1_[:], 1.0 / (2 * math.pi), ALU.mult, [tq], [tq])
                        sincos(s1, t1_[:], 128, 256, sn[:], cs[:], [tq], [tq], "C_sc1")
                        b.tt("dve", t2_[:], lr[:], ls[:], ALU.mult, [tq], [tq])
                        b.act(mg[:], t2_[:], AF.Exp, [tq], [tq])
                        b.tt("dve", cs[:], cs[:], mg[:], ALU.mult, [tq], [tq])
                        b.tt("dve", sn[:], sn[:], mg[:], ALU.mult, [tq], [tq])
                        b.ts("dve", cs[:], cs[:], -1.0, ALU.add, [tq], [tq])
                        b.tt("dve", t1_[:], lr[:], lr[:], ALU.mult, [tq], [tq])
                        b.tt("dve", t2_[:], li[:], li[:], ALU.mult, [tq], [tq])
                        b.tt("dve", t1_[:], t1_[:], t2_[:], ALU.add, [tq], [tq])
                        b.recip(t1_[:], t1_[:], [tq], [tq])
                        b.tt("dve", fre[:], cs[:], lr[:], ALU.mult, [tq], [tq])
                        b.tt("dve", t2_[:], sn[:], li[:], ALU.mult, [tq], [tq])
                        b.tt("dve", fre[:], fre[:], t2_[:], ALU.add, [tq], [tq])
                        b.tt("dve", fre[:], fre[:], t1_[:], ALU.mult, [tq], [tq])
                        b.tt("dve", fim[:], sn[:], lr[:], ALU.mult, [tq], [tq])
                        b.tt("dve", t2_[:], cs[:], li[:], ALU.mult, [tq], [tq])
                        b.tt("dve", fim[:], fim[:], t2_[:], ALU.subtract, [tq], [tq])
                        b.tt("dve", fim[:], fim[:], t1_[:], ALU.mult, [tq], [tq])
                        b.tt("dve", bbr[:], fre[:], Br[:], ALU.mult, [tq], [tq])
                        b.tt("dve", t2_[:], fim[:], Bi[:], ALU.mult, [tq], [tq])
                        b.tt("dve", bbr[:], bbr[:], t2_[:], ALU.subtract, [tq], [tq])
                        b.tt("dve", bbi[:], fre[:], Bi[:], ALU.mult, [tq], [tq])
                        b.tt("dve", t2_[:], fim[:], Br[:], ALU.mult, [tq], [tq])
                        b.tt("dve", bbi[:], bbi[:], t2_[:], ALU.add, [tq], [tq])
                        m8 = sb(s1, "C_m8", [128, 8], F32)
                        hm = sb(s1, "C_hm", [128, 2], F32)
                        b.memset("pool", m8[:], 1.0, [tq])
                        b.memset("pool", hm[:], 1.0, [tq])
                        b.asel(m8[:], m8[:], [[-16, 8]], ALU.is_ge, 0.0, 0, 1, [], [tq])
                        b.asel(m8[:], m8[:], [[16, 8]], ALU.is_ge, 0.0, 15, -1, [], [tq])
                        b.asel(hm[:], hm[:], [[-64, 2]], ALU.is_ge, 0.0, 0, 1, [], [tq])
                        b.asel(hm[:], hm[:], [[64, 2]], ALU.is_ge, 0.0, 63, -1, [], [tq])
                        for jj in range(4):
                            for gl in range(2):
                                k0 = 2 * jj + gl
                                for (dst, src_) in ((BreT, bbr), (BimT, bbi)):
                                    b.ts("dve", dst[:, jj:16:4, gl * 64:(gl + 1) * 64], src_[:, :].rearrange("p (o q) -> p o q", q=64),
                                         m8[:, k0:k0 + 1], ALU.mult, [tq], [tW])
                        Cst = [sb(s1, "C_Cst%d" % i, [128, 16, 16], F32) for i in range(2)]
                        b.dma(Cst[0][:], ssmC[L, 0], w=[tq])
                        b.dma(Cst[1][:], ssmC[L, 1], w=[tq])
                        for dst in (CreT, nCreT, nCimT):
                            b.memset("pool", dst[:], 0.0, [tW])
                        for jj in range(4):
                            for gl in range(2):
                                k0 = 2 * jj + gl
                                for (dst, src_, sgn) in ((CreT, Cst[0], 1.0), (nCreT, Cst[0], -1.0), (nCimT, Cst[1], -1.0)):
                                    b.ts("dve", dst[:, jj:16:4, k0 * 16:(k0 + 1) * 16], src_[:, jj:16:4, :], hm[:, gl:gl + 1], ALU.mult,
                                         [tq], [tW], s2=sgn, op1=ALU.mult)
                        p2 = [sb(s1, "C_p2%d" % i, [128, 16], F32) for i in range(4)]
                        for i in range(3):
                            b.dma(p2[i][:], ssmL2[L, i], w=[tq])
                        b.act(p2[2][:], p2[2][:], AF.Exp, [tq], [tq])
                        b.tt("dve", p2[3][:], p2[0][:], p2[2][:], ALU.mult, [tq], [tq])
                        b.act(mag2[:], p2[3][:], AF.Exp, [tq], [tP2])
                        b.tt("dve", thn2[:], p2[1][:], p2[2][:], ALU.mult, [tq], [tP2])
                        b.ts("dve", thn2[:], thn2[:], 1.0 / (2 * math.pi), ALU.mult, [tP2], [tP2])
                        b.ts("dve", p2[3][:], thn2[:], float(TT), ALU.mult, [tP2, tq], [tq])
                        sincos(s1, p2[3][:], 128, 16, sr2[:], cr2[:], [tq], [tP2], "C_sc2")
                        b.ts("dve", nsr2[:], sr2[:], -1.0, ALU.mult, [tP2], [tP2])
                        ioi = sb(s1, "C_ioi", [128, TT], I32)
                        b.iota(ioi[:], [[1, TT]], [], [tio])
                        b.cp("pool", iof[:], ioi[:], [tio], [tio])
                    S.barrier()
                    bg_attach(s0)
                    bgs["eng"], bgs["qs"] = "act", "act"
                    uo = [sb(s0, "C_uo%d" % i, [128, T], BF16) for i in range(2)]
                    tuo = toks(2)
                    tabs = [[sb(s0, "C_tab%d_%d" % (i, k), [128, 2, TT], F32) for k in range(4)] for i in range(2)]
                    ttab = [toks(4) for _ in range(2)]
                    ub = sb(s0, "C_ub", [128, TT], F32)
                    tub = Tk()
                    aa = [[sb(s0, "C_a%d_%d" % (i, k), [128, TT], F32) for k in range(4)] for i in range(3)]
                    taa = [toks(4) for _ in range(3)]
                    bp = [[sb(s0, "C_bp%d_%d" % (i, k), [128, TT], F32) for k in range(2)] for i in range(3)]
                    tbp = [toks(2) for _ in range(3)]
                    ww = [[sb(s0, "C_w%d_%d" % (i, k), [128, TT], F32) for k in range(2)] for i in range(3)]
                    tww = [toks(2) for _ in range(3)]
                    pp = [[sb(s0, "C_p%d_%d" % (i, k), [128, TT], BF16) for k in range(4)] for i in range(3)]
                    tpp = [toks(4) for _ in range(3)]
                    car = sb(s0, "C_car", [128, 2, 16], F32)
                    tcar = toks(16)
                    ctmp = sb(s0, "C_ctmp", [128, 4, 16], F32)
                    ucand = [sb(s0, "C_ucand%d" % i, [128, T], BF16) for i in range(2)]
                    tucand = toks(2)
                    yo = [sb(s0, "C_yo%d" % i, [128, TT], F32) for i in range(2)]
                    tyo = toks(2)
                    b.memset("pool", car[:], 0.0, tcar)
                    it = [0]
                    for o in range(4):
                        ui = o % 2
                        for cand in range(2):
                            row = 2240 + (4 * cand + o) * 128
                            for g_ in range(8):
                                for (zr, zn, zo) in zg_pieces(RG[g_], row, 128):
                                    b.dma(ucand[cand][zo:zo + zn, g_ * TT:(g_ + 1) * TT],
                                          zFg[zr:zr + zn, LG[g_] * TT:(LG[g_] + 1) * TT], r=tzg, w=[tucand[cand]])
                        b.ts("dve", uo[ui][:], ucand[0][:], selv[:, 0:1], ALU.mult, [tucand[0], tconst], [tuo[ui]])
                        b.stt(uo[ui][:], ucand[1][:], selv[:, 1:2], uo[ui][:], ALU.mult, ALU.add, [tucand[1], tconst, tuo[ui]], [tuo[ui]])
                        for jl in range(4):
                            j = 4 * o + jl
                            b.ts("dve", ub[:], iof[:], thn2[:, j:j + 1], ALU.mult, [tio, tP2], [tub])
                            sincos(s0, ub[:], 128, TT, tabs[ui][jl][:, 1, :], tabs[ui][jl][:, 0, :], [tub], [ttab[ui][jl]],
                                   "C_sc3", eng="dve")
                        psteps4 = [(tt_, jl) for tt_ in range(NT) for jl in range(4)]

                        def issueB(k):
                            tt_, jl = psteps4[k]
                            bi = k % 3
                            j = 4 * o + jl
                            b.mm(ps[2 * bi][:], BreT[:, j, :], uo[ui][:, tt_ * TT:(tt_ + 1) * TT], True, True, [tW, tuo[ui]], [tps[2 * bi]])
                            b.mm(ps[2 * bi + 1][:], BimT[:, j, :], uo[ui][:, tt_ * TT:(tt_ + 1) * TT], True, True, [tW, tuo[ui]], [tps[2 * bi + 1]])
                        issueB(0)
                        issueB(1)
                        for k4, (tt_, jl) in enumerate(psteps4):
                            t0 = tt_ * TT
                            ypo, typo = (ps[6], tps[6]) if tt_ % 2 == 0 else (psx, tpsx)
                            j = 4 * o + jl
                            bi = k4 % 3
                            pA, pB = ps[2 * bi], ps[2 * bi + 1]
                            tA, tB = tps[2 * bi], tps[2 * bi + 1]
                            cosT, sinT = tabs[ui][jl][:, 0, :], tabs[ui][jl][:, 1, :]
                            ttb = ttab[ui][jl]
                            a_, ta_ = aa[bi], taa[bi]
                            b.tt("dve", a_[0][:], pA[:], cosT, ALU.mult, [tA, ttb], [ta_[0]])
                            b.tt("dve", a_[1][:], pB[:], sinT, ALU.mult, [tB, ttb], [ta_[1]])
                            b.tt("dve", a_[2][:], pB[:], cosT, ALU.mult, [tB, ttb], [ta_[2]])
                            b.tt("dve", a_[3][:], pA[:], sinT, ALU.mult, [tA, ttb], [ta_[3]])
                            if k4 + 2 < len(psteps4):
                                issueB(k4 + 2)
                            b.tt("pool", bp[bi][0][:], a_[0][:], a_[1][:], ALU.add, [ta_[0], ta_[1]], [tbp[bi][0]])
                            b.tt("dve", bp[bi][1][:], a_[2][:], a_[3][:], ALU.subtract, [ta_[2], ta_[3]], [tbp[bi][1]])
                            for ri in range(2):
                                b.scan(ww[bi][ri][:], mag2[:, j:j + 1].to_broadcast([128, TT]), bp[bi][ri][:], car[:, ri, j:j + 1],
                                       [tP2, tbp[bi][ri], tcar[j]], [tww[bi][ri]])
                            wr_l, wi_l = ww[bi][0][:, TT - 1:TT], ww[bi][1][:, TT - 1:TT]
                            c4 = [ctmp[:, k, j:j + 1] for k in range(4)]
                            crj, srj, nsrj = cr2[:, j:j + 1], sr2[:, j:j + 1], nsr2[:, j:j + 1]
                            tc = tcar[j]
                            b.act(c4[0], wr_l, AF.Copy, [tww[bi][0], tP2], [tc], scale=crj)
                            b.act(car[:, 0, j:j + 1], wi_l, AF.Identity, [tww[bi][1], tP2, tc], [tc], scale=nsrj, bias=c4[0])
                            b.act(c4[1], wr_l, AF.Copy, [tww[bi][0], tP2], [tc], scale=srj)
                            b.act(car[:, 1, j:j + 1], wi_l, AF.Identity, [tww[bi][1], tP2, tc], [tc], scale=crj, bias=c4[1])
                            p_, tp_ = pp[bi], tpp[bi]
                            b.tt("pool", p_[0][:], ww[bi][0][:], cosT, ALU.mult, [tww[bi][0], ttb], [tp_[0]])
                            b.tt("dve", p_[1][:], ww[bi][1][:], sinT, ALU.mult, [tww[bi][1], ttb], [tp_[1]])
                            b.tt("dve", p_[2][:], ww[bi][1][:], cosT, ALU.mult, [tww[bi][1], ttb], [tp_[2]])
                            b.tt("dve", p_[3][:], ww[bi][0][:], sinT, ALU.mult, [tww[bi][0], ttb], [tp_[3]])
                            for k, lh in enumerate((CreT, nCreT, nCimT, nCimT)):
                                b.mm(ypo[:], lh[:, j, :], p_[k][:], jl == 0 and k == 0, jl == 3 and k == 3,
                                     [tW, tp_[k]], [typo])
                            bg_pump(1)
                            if jl == 3:
                                yi = tt_ % 2
                                b.stt(yo[yi][:], uo[ui][:, t0:t0 + TT], vcol(L, 64 + o), ypo[:], ALU.mult, ALU.add,
                                      [tuo[ui], typo, tconst], [tyo[yi]])
                                b.dma(yprel[o * 128:(o + 1) * 128, t0:t0 + TT], yo[yi][:], r=[tyo[yi]])
                    bg_detach(all_=True)
                S.barrier()
                for o_ in range(4):
                    b.allgather(ypre[o_ * 256:(o_ + 1) * 256, :], yprel[o_ * 128:(o_ + 1) * 128, :])
                S.barrier()

            if stages is None or "5a" in stages:
                with ExitStack() as s0:
                    wl = sb(s0, "g_wl", [128, 8, 1024], BF16)
                    twl = Tk()
                    yp = [sb(s0, "g_yp%d" % i, [128, 8, TT], F32) for i in range(2)]
                    typ = [toks(8) for _ in range(2)]
                    yg = sb(s0, "g_yg", [128, 8, TT], BF16)
                    tyg = toks(8)
                    tmp = [sb(s0, "g_tmp%d" % i, [128, TT], F32) for i in range(4)]
                    ttmp = toks(4)
                    sg = [sb(s0, "g_sg%d" % i, [128, TT], BF16) for i in range(2)]
                    tsg = toks(2)
                    yo = [sb(s0, "g_yo%d" % i, [128, TT], BF16) for i in range(2)]
                    tyo = toks(2)
                    b.dma(wl[:], Wl[:, :].rearrange("(k p) n -> p k n", p=128), w=[twl])
                    yq = [sb(s0, "g_yq%d" % i, [128, 8, TT], F32) for i in range(2)]
                    tyq = toks(2)
                    for tt_ in range(NTL):
                        t0 = tt_ * TT
                        bi = tt_ % 2
                        for cand in range(2):
                            g0_ = G[cand][tt_] * TT
                            for c in range(8):
                                ro = (c % 4) * 256 + (c // 4) * 128
                                b.dma(yq[cand][:, c, :], ypre[ro:ro + 128, g0_:g0_ + TT], w=[tyq[cand]])
                        for c in range(8):
                            b.ts("dve", yp[bi][:, c, :], yq[0][:, c, :], selv[:, 0:1], ALU.mult, [tyq[0], tconst], [typ[bi][c]])
                            b.stt(yp[bi][:, c, :], yq[1][:, c, :], selv[:, 1:2], yp[bi][:, c, :], ALU.mult, ALU.add,
                                  [tyq[1], tconst, typ[bi][c]], [typ[bi][c]])
                        for c in range(8):
                            a, a2 = (2 * c) % 4, (2 * c + 1) % 4
                            b.act(tmp[a][:], yp[bi][:, c, :], AF.Square, [typ[bi][c]], [ttmp[a]])
                            b.ts("dve", tmp[a][:], tmp[a][:], 0.044715, ALU.mult, [ttmp[a]], [ttmp[a]], s2=1.0, op1=ALU.add)
                            b.tt(("dve", "pool")[c % 2], tmp[a][:], tmp[a][:], yp[bi][:, c, :], ALU.mult, [ttmp[a], typ[bi][c]], [ttmp[a]])
                            b.act(tmp[a2][:], tmp[a][:], AF.Sigmoid, [ttmp[a]], [ttmp[a2]], scale=1.5957691216057308)
                            b.tt("dve", yg[:, c, :], yp[bi][:, c, :], tmp[a2][:], ALU.mult, [typ[bi][c], ttmp[a2]], [tyg[c]])
                        for m in range(8):
                            pi_ = m % 4
                            for c in range(8):
                                b.mm(ps[pi_][:], wl[:, c, m * 128:(m + 1) * 128], yg[:, c, :], c == 0, c == 7,
                                     [twl, tyg[c]], [tps[pi_]])
                            si = m % 2
                            b.act(sg[si][:], ps[pi_][:], AF.Sigmoid, [tps[pi_]], [tsg[si]])
                            b.tt(("dve", "pool")[m % 2], yo[si][:], yg[:, m, :], sg[si][:], ALU.mult, [tyg[m], tsg[si]], [tyo[si]])
                            b.dma(ycat[1024 + m * 128:1024 + (m + 1) * 128, t0:t0 + TT], yo[si][:], r=[tyo[si]])
                S.barrier()

            if stages is None or "5b" in stages:
                last = (L == n_layers - 1)
                src_h = xT if L == 0 else hT
                with ExitStack() as s0:
                    ht = sb(s0, "d_ht", [128, KC, TT], F32)
                    hb = sb(s0, "d_hb", [128, KC, TT], BF16)
                    yc = hb
                    hid = sb(s0, "d_hid", [128, 64, TT], BF16)
                    tht, thb, thid = toks(KC), toks(KC), toks(64)
                    NWB = 3
                    wt = [sb(s0, "d_wt%d" % i, [128, 8192], BF16) for i in range(NWB)]
                    twt = toks(NWB)
                    wp = sb(s0, "d_wp", [128, 2, D], BF16)
                    twp = Tk()
                    pf_ = sb(s0, "d_pf", [128, 2, TT], F32)
                    pb_ = sb(s0, "d_pb", [128, 2, TT], BF16)
                    tpf, tpb = Tk(), Tk()
                    tmp = [sb(s0, "d_tmp%d" % i, [128, TT], F32) for i in range(4)]
                    ttmp = toks(4)
                    tmpb = [sb(s0, "d_tmpb%d" % i, [128, TT], BF16) for i in range(4)]
                    ttmpb = toks(4)
                    r2 = sb(s0, "d_r2", [128, TT], F32)
                    tr2 = Tk()
                    rs = rstd_from(s0, "d_rs")
                    b.dma(wp[:], Wp[:, :].rearrange("(k p) n -> p k n", p=128), w=[twp])
                    wcnt = [0]
                    tmpc = [0]

                    for tt_ in range(NTL):
                        t0 = tt_ * TT
                        pieces = []
                        for c in range(4):
                            pieces.append(("o", Wo[:, c * 512:(c + 1) * 512].rearrange("(k p) n -> p k n", p=128), 16, 512, c))
                        for c in range(16):
                            pieces.append(("u", Wu[:, c * 512:(c + 1) * 512].rearrange("(k p) n -> p k n", p=128), 16, 512, c))
                        for c in range(8):
                            for hf in range(2):
                                pieces.append(("d", Wd[hf * 4096:(hf + 1) * 4096, c * 256:(c + 1) * 256].rearrange("(k p) n -> p k n", p=128), 32, 256, (c, hf)))
                        for c in range(4):
                            pieces.append(("g", Wg[:, c * 512:(c + 1) * 512].rearrange("(k p) n -> p k n", p=128), 16, 512, c))
                        slots = {}

                        def issue(pi_):
                            kind, src, k_, n_, _ = pieces[pi_]
                            i = wcnt[0] % NWB
                            wcnt[0] += 1
                            b.dma(wt[i][:, 0:k_ * n_].rearrange("p (k n) -> p k n", k=k_), src, w=[twt[i]])
                            slots[pi_] = i

                        for g4 in range(4):
                            b.dma(ht[:, 4 * g4:4 * g4 + 4, :],
                                  src_h[g4 * 512:(g4 + 1) * 512, t0:t0 + TT].rearrange("(k p) t -> p k t", p=128),
                                  w=tht[4 * g4:4 * g4 + 4])
                            b.dma(yc[:, 4 * g4:4 * g4 + 4, :],
                                  ycat[g4 * 512:(g4 + 1) * 512, t0:t0 + TT].rearrange("(k p) t -> p k t", p=128),
                                  w=thb[4 * g4:4 * g4 + 4])
                        b.dma(pf_[:], pT[L, :, t0:t0 + TT].rearrange("(k p) t -> p k t", p=128), w=[tpf])
                        issue(0)
                        issue(1)
                        b.cp("pool", pb_[:], pf_[:], [tpf], [tpb])
                        rB = trB = None
                        psi = 0
                        for pi_, (kind, src, k_, n_, cinfo) in enumerate(pieces):
                            if pi_ + 2 < len(pieces):
                                issue(pi_ + 2)
                            wi_ = slots[pi_]
                            wv = wt[wi_][:, 0:k_ * n_].rearrange("p (k n) -> p k n", k=k_)
                            if kind == "o":
                                for mb in range(4):
                                    m = cinfo * 4 + mb
                                    pa = psi % 5
                                    psi += 1
                                    for kc in range(KC):
                                        b.mm(ps[pa][:], wv[:, kc, mb * 128:(mb + 1) * 128], yc[:, kc, :], kc == 0, kc == KC - 1,
                                             [twt[wi_], thb[kc]], [tps[pa]])
                                    b.tt("dve", ht[:, m, :], ht[:, m, :], ps[pa][:], ALU.add, [tht[m], tps[pa]], [tht[m]])
                                if cinfo == 3:
                                    rB, trB = rs(ht, tht, hb, thb, 6)
                                    b.tt("pool", r2[:], rB[:], rB[:], ALU.mult, [trB], [tr2])
                            elif kind == "u":
                                for mb in range(4):
                                    f = cinfo * 4 + mb
                                    pa = psi % 5
                                    psi += 1
                                    for kc in range(KC):
                                        b.mm(ps[pa][:], wv[:, kc, mb * 128:(mb + 1) * 128], hb[:, kc, :], kc == 0, kc == KC - 1,
                                             [twt[wi_], thb[kc]], [tps[pa]])
                                    tb_ = f % 4
                                    b.act(tmpb[tb_][:], ps[pa][:], AF.Relu, [tps[pa]], [ttmpb[tb_]])
                                    b.tt(("dve", "pool")[f % 2], hid[:, f, :], tmpb[tb_][:], tmpb[tb_][:], ALU.mult, [ttmpb[tb_]], [thid[f]])
                            elif kind == "d":
                                c, hf = cinfo
                                if hf == 0:
                                    dpa = [psi % 5, (psi + 1) % 5]
                                    psi += 2
                                for mb in range(2):
                                    pa = dpa[mb]
                                    for k in range(32):
                                        f = hf * 32 + k
                                        b.mm(ps[pa][:], wv[:, k, mb * 128:(mb + 1) * 128], hid[:, f, :], f == 0, f == 63,
                                             [twt[wi_], thid[f]], [tps[pa]])
                                    if hf == 1:
                                        m = c * 2 + mb
                                        ti = tmpc[0] % 4
                                        tmpc[0] += 1
                                        b.tt("dve", tmp[ti][:], ps[pa][:], r2[:], ALU.mult, [tps[pa], tr2], [ttmp[ti]])
                                        b.tt("pool", ht[:, m, :], ht[:, m, :], tmp[ti][:], ALU.add, [tht[m], ttmp[ti]], [tht[m]])
                                if c == 7 and hf == 1:
                                    rB, trB = rs(ht, tht, hb, thb, 6)
                            elif kind == "g":
                                for mb in range(4):
                                    m = cinfo * 4 + mb
                                    pa, pb2 = psi % 5, (psi + 1) % 5
                                    psi += 2
                                    for kc in range(KC):
                                        b.mm(ps[pa][:], wv[:, kc, mb * 128:(mb + 1) * 128], hb[:, kc, :], kc == 0, kc == KC - 1,
                                             [twt[wi_], thb[kc]], [tps[pa]])
                                    for k in range(2):
                                        b.mm(ps[pb2][:], wp[:, k, m * 128:(m + 1) * 128], pb_[:, k, :], k == 0, k == 1,
                                             [twp, tpb], [tps[pb2]])
                                    ti, tj = tmpc[0] % 4, (tmpc[0] + 1) % 4
                                    tmpc[0] += 2
                                    b.tt("dve", tmp[ti][:], ps[pa][:], rB[:], ALU.mult, [tps[pa], trB], [ttmp[ti]])
                                    b.act(tmp[tj][:], tmp[ti][:], AF.Sigmoid, [ttmp[ti]], [ttmp[tj]])
                                    b.tt("dve", tmp[ti][:], tmp[tj][:], ps[pb2][:], ALU.mult, [ttmp[tj], tps[pb2]], [ttmp[ti]])
                                    b.tt("pool", ht[:, m, :], ht[:, m, :], tmp[ti][:], ALU.add, [tht[m], ttmp[ti]], [tht[m]])
                        if not last:
                            for g4 in range(4):
                                b.dma(hT[g4 * 512:(g4 + 1) * 512, t0:t0 + TT].rearrange("(k p) t -> p k t", p=128),
                                      ht[:, 4 * g4:4 * g4 + 4, :], r=tht[4 * g4:4 * g4 + 4])
                        else:
                            rB, trB = rs(ht, tht, hb, thb, 6)
                            for m in range(KC):
                                b.stt(ht[:, m, :], ht[:, m, :], vcol(L, 48 + m), rB[:], ALU.mult, ALU.mult,
                                      [tht[m], trB, tconst], [tht[m]])
                            for g4 in range(4):
                                b.dma(outT[g4 * 512:(g4 + 1) * 512, t0:t0 + TT].rearrange("(k p) t -> p k t", p=128),
                                      ht[:, 4 * g4:4 * g4 + 4, :], r=tht[4 * g4:4 * g4 + 4], final=True)
                S.barrier()

        if not S.final:
            fin = sb(st, "fin", [128, 8], F32)
            tf = Tk()
            b.memset("dve", fin[:], 0.0, [tf])
            b.dma(outT[0:128, 0:8], fin[:], r=[tf], final=True)
        S.emit(st)
    return nc


def host_inputs(inputs, bidx, rank):
    f32 = np.float32
    x = np.asarray(inputs["x"], f32)
    p = np.asarray(inputs["p"], f32)
    tok = np.concatenate([np.arange(g * TT, (g + 1) * TT) for g in G[rank]])
    m = {}
    m["xT"] = np.ascontiguousarray(x[bidx][tok].T)
    m["pT"] = np.ascontiguousarray(np.transpose(p[:, bidx][:, tok], (0, 2, 1)))
    pos = np.asarray(inputs["positions"], np.int32)[bidx][tok]
    m["posf"] = np.ascontiguousarray(np.broadcast_to(pos[None, :], (64, TL)))
    m["qidxB"] = np.ascontiguousarray(np.broadcast_to(tok.astype(f32)[None, :], (128, TL)))
    m["qcol"] = np.ascontiguousarray(tok.astype(f32).reshape(16, 128).T)
    sel = np.zeros((128, 2), f32)
    sel[:, rank] = 1.0
    m["selv"] = sel
    fr = np.zeros((64, 2), f32)
    fr[:, 0] = (np.float32(10000.0) ** (-np.arange(0, 128, 2, dtype=f32) / f32(128))).astype(f32)
    fr[:32, 1] = (np.float32(10000.0) ** (-np.arange(0, 64, 2, dtype=f32) / f32(64))).astype(f32)
    m["freqs"] = fr
    m["w_in"] = np.asarray(inputs["w_in"], f32)
    m["w_out"] = np.asarray(inputs["w_out"], f32)
    m["w_up"] = np.asarray(inputs["w_up"], f32)
    m["w_down"] = np.asarray(inputs["w_down"], f32)
    m["w_gate"] = np.asarray(inputs["w_ple_gate"], f32)
    m["w_ple"] = np.asarray(inputs["w_ple_proj"], f32)
    m["w_glu"] = np.asarray(inputs["ssm_w_glu"], f32)
    vecs = np.zeros((2, 128, NVEC), f32)
    for l in range(2):
        vecs[l, :, 0:16] = np.asarray(inputs["norm_mix_g"], f32)[l].reshape(16, 128).T
        vecs[l, :, 16:32] = np.asarray(inputs["norm_mlp_g"], f32)[l].reshape(16, 128).T
        vecs[l, :, 32:48] = np.asarray(inputs["norm_ple_g"], f32)[l].reshape(16, 128).T
        vecs[l, :, 48:64] = np.asarray(inputs["final_g"], f32).reshape(16, 128).T
        vecs[l, :, 64:68] = np.asarray(inputs["ssm_D"], f32)[l].reshape(8, 128)[4 * rank:4 * rank + 4].T
        vecs[l, :, 72] = np.asarray(inputs["diff_subln_g"], f32)[l]
    m["vecs"] = vecs
    lam = np.stack([np.asarray(inputs[k], f32) for k in ("diff_lq1", "diff_lk1", "diff_lq2", "diff_lk2")], axis=1)
    m["lamv"] = np.ascontiguousarray(np.broadcast_to(lam[:, None], (2, 128, 4, 64)))
    gs = slice(32 * rank, 32 * rank + 32)
    lr = np.asarray(inputs["ssm_lambda_re"], f32)[:, gs]
    li = np.asarray(inputs["ssm_lambda_im"], f32)[:, gs]
    ls = np.asarray(inputs["ssm_log_step"], f32)[:, gs]
    ls_full = np.broadcast_to(ls[:, :, None], lr.shape)
    L1 = []
    L2 = []
    for a in (lr, li, ls_full):
        a1 = np.broadcast_to(a.reshape(2, 4, 8, 1, 64), (2, 4, 8, 16, 64))
        L1.append(np.transpose(a1, (0, 2, 3, 1, 4)).reshape(2, 128, 4, 64))
        a2 = a.reshape(2, 16, 2, 64)
        L2.append(np.transpose(a2, (0, 2, 3, 1)).reshape(2, 128, 16))
    m["ssmL1"] = np.ascontiguousarray(np.stack(L1, axis=1))
    m["ssmL2"] = np.ascontiguousarray(np.stack(L2, axis=1))
    Bs = []
    for k in ("ssm_B_re", "ssm_B_im"):
        a = np.asarray(inputs[k], f32)[:, gs]
        a = a.reshape(2, 4, 8, 64, 16)
        Bs.append(np.transpose(a, (0, 2, 4, 1, 3)).reshape(2, 128, 4, 64))
    m["ssmB"] = np.ascontiguousarray(np.stack(Bs, axis=1))
    Cs = []
    for k in ("ssm_C_re", "ssm_C_im"):
        a = np.asarray(inputs[k], f32)[:, gs]
        a = a.reshape(2, 16, 2, 16, 64)
        Cs.append(np.transpose(a, (0, 2, 4, 1, 3)).reshape(2, 128, 16, 16))
    m["ssmC"] = np.ascontiguousarray(np.stack(Cs, axis=1))
    return m


_NC_CACHE = {}


def kernel(**inputs):
    if "nc" not in _NC_CACHE:
        _NC_CACHE["nc"] = build()
    nc = _NC_CACHE["nc"]
    in_maps = [host_inputs(inputs, c // 2, c % 2) for c in range(8)]
    res = run_bass_kernel_spmd(nc, in_maps, core_ids=list(range(8)))
    out = np.zeros((4, T, D), np.float32)
    for c in range(8):
        tok = np.concatenate([np.arange(g * TT, (g + 1) * TT) for g in G[c % 2]])
        out[c // 2][tok] = res.results[c]["outT"].T
    return out
```
